# Optimizing a Trainium2 kernel written in Bass

```python
import math
import jax, jax.numpy as jnp
from jax import lax
import numpy as np

D_MODEL = 1024
BATCH = 16
SEQ = 2048
DEPTH = 1
DEC_BATCH = 8
DEC_SEQ = 4096
PAST_LEN = 128

HEAD_DIM = 64
GRID_W = 64
Q_BLOCK = 128
ROPE_THETA = 10000.0
EPS = 1e-6
A_HEADS = D_MODEL // (2 * HEAD_DIM)
A_KV_HEADS = max(1, A_HEADS // 4)
A_WIDTH = A_HEADS * HEAD_DIM
A_KV = A_KV_HEADS * HEAD_DIM
B_HEADS = D_MODEL // (4 * HEAD_DIM)
B_QK = B_HEADS * 2 * HEAD_DIM
B_WIDTH = B_HEADS * 2 * HEAD_DIM
MIX_WIDTH = A_WIDTH + B_WIDTH
IN_SIZES = (A_WIDTH, A_KV, A_KV, A_WIDTH, B_QK, B_QK, B_WIDTH, B_WIDTH)
IN_WIDTH = sum(IN_SIZES)
IN_SPLITS = tuple(int(v) for v in np.cumsum(IN_SIZES)[:-1])

kernel_name = "hybrid_gqa_axial_diffattn_encoder"


def rmsnorm(x, g):
    xf = x.astype(jnp.float32)
    y = xf * lax.rsqrt(jnp.mean(xf * xf, axis=-1, keepdims=True) + EPS)
    return (y * g.astype(jnp.float32)).astype(x.dtype)


def rope(x, pos):
    dim = x.shape[-1]
    inv = ROPE_THETA ** (-jnp.arange(0, dim, 2, dtype=jnp.float32) / dim)
    ang = pos[:, None] * inv[None, :]
    shape = (ang.shape[0],) + (1,) * (x.ndim - 3) + (dim // 2,)
    c = jnp.cos(ang).reshape(shape)
    s = jnp.sin(ang).reshape(shape)
    xf = x.astype(jnp.float32)
    x1, x2 = xf[..., : dim // 2], xf[..., dim // 2:]
    return jnp.concatenate([x1 * c - x2 * s, x2 * c + x1 * s], axis=-1).astype(x.dtype)


def axial_rope(x, row, col):
    half = x.shape[-1] // 2
    return jnp.concatenate([rope(x[..., :half], row), rope(x[..., half:], col)], axis=-1)


def query_blocks(fn, q):
    B, S = q.shape[0], q.shape[1]
    nb = S // Q_BLOCK
    qb = jnp.swapaxes(q.reshape((B, nb, Q_BLOCK) + q.shape[2:]), 0, 1)
    out = lax.map(fn, qb)
    out = jnp.swapaxes(out, 0, 1)
    return out.reshape((B, S) + out.shape[3:])


def gqa_attention(q, k, v):
    B = q.shape[0]
    G = A_HEADS // A_KV_HEADS
    scale = HEAD_DIM ** -0.5

    def blk(qb):
        qb = qb.reshape(B, Q_BLOCK, A_KV_HEADS, G, HEAD_DIM)
        s = jnp.einsum('bqhgd,bkhd->bhgqk', qb, k, preferred_element_type=jnp.float32) * scale
        p = jax.nn.softmax(s, axis=-1).astype(v.dtype)
        o = jnp.einsum('bhgqk,bkhd->bqhgd', p, v)
        return o.reshape(B, Q_BLOCK, A_HEADS, HEAD_DIM)

    return query_blocks(blk, q)


def diff_attention(q, k, v, lam):
    scale = HEAD_DIM ** -0.5

    def blk(qb):
        s = jnp.einsum('bqhmd,bkhmd->bhmqk', qb, k, preferred_element_type=jnp.float32) * scale
        p = jax.nn.softmax(s, axis=-1)
        w = (p[:, :, 0] - lam * p[:, :, 1]).astype(v.dtype)
        return jnp.einsum('bhqk,bkhe->bqhe', w, v)

    return query_blocks(blk, q)


def mixer_layer(x, l, g_norm, w_in, a_q_norm, a_k_norm, b_lambda_q1, b_lambda_k1,
                b_lambda_q2, b_lambda_k2, b_subln, w_out):
    Bsz, S, _ = x.shape
    rows = S // GRID_W
    row = jnp.repeat(jnp.arange(rows, dtype=jnp.float32), GRID_W)
    col = jnp.tile(jnp.arange(GRID_W, dtype=jnp.float32), rows)
    pos = jnp.arange(S, dtype=jnp.float32)

    h = rmsnorm(x, g_norm[l])
    proj = jnp.einsum('bsd,de->bse', h, w_in[l])
    qa, ka, va, ga, qb, kb, vb, gb = jnp.split(proj, IN_SPLITS, axis=-1)

    qa = rmsnorm(qa.reshape(Bsz, S, A_HEADS, HEAD_DIM), a_q_norm[l])
    ka = rmsnorm(ka.reshape(Bsz, S, A_KV_HEADS, HEAD_DIM), a_k_norm[l])
    va = va.reshape(Bsz, S, A_KV_HEADS, HEAD_DIM)
    qa = axial_rope(qa, row, col)
    ka = axial_rope(ka, row, col)
    oa = gqa_attention(qa, ka, va).reshape(Bsz, S, A_WIDTH) * jax.nn.silu(ga)

    lam_init = 0.8 - 0.6 * math.exp(-0.3 * l)
    f32 = jnp.float32
    lam = (jnp.exp(jnp.sum(b_lambda_q1[l].astype(f32) * b_lambda_k1[l].astype(f32)))
           - jnp.exp(jnp.sum(b_lambda_q2[l].astype(f32) * b_lambda_k2[l].astype(f32)))
           + lam_init)
    qb = rope(qb.reshape(Bsz, S, B_HEADS, 2, HEAD_DIM), pos)
    kb = rope(kb.reshape(Bsz, S, B_HEADS, 2, HEAD_DIM), pos)
    vb = vb.reshape(Bsz, S, B_HEADS, 2 * HEAD_DIM)
    ob = diff_attention(qb, kb, vb, lam)
    ob = rmsnorm(ob, b_subln[l]) * (1.0 - lam_init)
    ob = ob.reshape(Bsz, S, B_WIDTH) * jax.nn.silu(gb)

    mixed = jnp.concatenate([oa, ob], axis=-1)
    return x + jnp.einsum('bse,ed->bsd', mixed, w_out[l])


def setup_inputs(seed: int = 0) -> dict:
    key = jax.random.key(seed)
    ks = jax.random.split(key, 14)
    f = jnp.float32
    nrm = lambda k, shape, s: jax.random.normal(k, shape, f) * s
    return {
        "x_prompt": nrm(ks[0], (BATCH, SEQ, D_MODEL), 1.0),
        "x_sample": nrm(ks[1], (DEC_BATCH, DEC_SEQ, D_MODEL), 1.0),
        "g_norm": 1.0 + nrm(ks[2], (DEPTH, D_MODEL), 0.02),
        "w_in": nrm(ks[3], (DEPTH, D_MODEL, IN_WIDTH), D_MODEL ** -0.5),
        "a_q_norm": 1.0 + nrm(ks[4], (DEPTH, HEAD_DIM), 0.02),
        "a_k_norm": 1.0 + nrm(ks[5], (DEPTH, HEAD_DIM), 0.02),
        "b_lambda_q1": nrm(ks[6], (DEPTH, HEAD_DIM), 0.1),
        "b_lambda_k1": nrm(ks[7], (DEPTH, HEAD_DIM), 0.1),
        "b_lambda_q2": nrm(ks[8], (DEPTH, HEAD_DIM), 0.1),
        "b_lambda_k2": nrm(ks[9], (DEPTH, HEAD_DIM), 0.1),
        "b_subln": 1.0 + nrm(ks[10], (DEPTH, 2 * HEAD_DIM), 0.02),
        "w_out": nrm(ks[11], (DEPTH, MIX_WIDTH, D_MODEL), MIX_WIDTH ** -0.5),
        "g_final": 1.0 + nrm(ks[12], (D_MODEL,), 0.02),
    }


def reference(x_prompt, x_sample, g_norm, w_in, a_q_norm, a_k_norm, b_lambda_q1, b_lambda_k1,
              b_lambda_q2, b_lambda_k2, b_subln, w_out, g_final):
    hp = x_prompt
    hs = x_sample
    for l in range(DEPTH):
        hp = mixer_layer(hp, l, g_norm, w_in, a_q_norm, a_k_norm, b_lambda_q1, b_lambda_k1,
                         b_lambda_q2, b_lambda_k2, b_subln, w_out)
        hs = mixer_layer(hs, l, g_norm, w_in, a_q_norm, a_k_norm, b_lambda_q1, b_lambda_k1,
                         b_lambda_q2, b_lambda_k2, b_subln, w_out)
    y_prompt = rmsnorm(hp, g_final)
    y_sample = rmsnorm(hs, g_final)
    return (y_prompt, y_sample)
```

```python
import contextlib
import numpy as np
import concourse.bass as bass
import concourse.mybir as mybir
from concourse.bass_utils import run_bass_kernel_spmd

F32 = mybir.dt.float32
BF16 = mybir.dt.bfloat16
ALU = mybir.AluOpType
AF = mybir.ActivationFunctionType

D_MODEL = 1024
HEAD_DIM = 64
GRID_W = 64
ROPE_THETA = 10000.0
EPS = 1e-6
N_CORES = 8
SEQS = (2048, 2048, 4096)
TOK = sum(SEQS)
SMAX = 4096
QB = 512
LAM_INIT = 0.8 - 0.6 * 1.0

ENGINES = ("pe", "act", "dve", "pool", "sp")


class Op:
    __slots__ = ("eng", "fn", "deps", "sig", "sem", "cnt", "idx", "dma_key", "raw")

    def __init__(self, eng, fn, deps, dma_key=None):
        self.raw = set(id(d) for d in deps)
        self.eng = eng
        self.fn = fn
        self.deps = deps
        self.sig = False
        self.sem = None
        self.cnt = 0
        self.idx = -1
        self.dma_key = dma_key


class Buf:
    __slots__ = ("name", "w", "r", "rd", "excl")

    def __init__(self, name, excl=False):
        self.name = name
        self.excl = excl
        self.w = None
        self.r = {}
        self.rd = []

    def readers(self):
        return list(self.r.values()) + self.rd


class Prog:
    def __init__(self, nc):
        self.nc = nc
        self.q = {e: [] for e in ENGINES}
        self.stack = contextlib.ExitStack()

    def sbuf(self, name, shape, dtype):
        return self.stack.enter_context(self.nc.sbuf_tensor(name, list(shape), dtype))

    def psum(self, name, shape, dtype):
        return self.stack.enter_context(self.nc.psum_tensor(name, list(shape), dtype))

    def _record(self, op, reads, writes):
        deps = op.deps
        raw = op.raw
        for b in reads:
            if b.w is not None:
                deps.append(b.w)
                raw.add(id(b.w))
            if b.excl:
                for en, r in b.r.items():
                    if en != op.eng:
                        deps.append(r)
        for b in writes:
            deps.extend(b.readers())
            if b.w is not None:
                deps.append(b.w)
        op.idx = len(self.q[op.eng])
        self.q[op.eng].append(op)
        for b in reads:
            if op.dma_key is not None:
                b.rd.append(op)
            else:
                b.r[op.eng] = op
        for b in writes:
            b.w = op
            b.r = {}
            b.rd = []
        return op

    def op(self, eng, fn, reads=(), writes=(), extra=()):
        return self._record(Op(eng, fn, [d for d in extra if d is not None]), reads, writes)

    def dma(self, eng, out, in_, key, reads=(), writes=(), extra=()):
        def fn(e, out=out, in_=in_):
            return e.dma_start(out=out, in_=in_)
        return self._record(Op(eng, fn, [d for d in extra if d is not None], dma_key=key),
                            reads, writes)

    def emit(self, self_gap=1 << 30):
        nc = self.nc
        st = self.stack
        for e in ENGINES:
            for op in self.q[e]:
                for d in op.deps:
                    if d.eng == e and d.dma_key is None and e == "pe":
                        continue
                    d.sig = True
                if op.dma_key is not None:
                    op.sig = True
        esem = {e: st.enter_context(nc.semaphore("sem_" + e)) for e in ENGINES}
        dsem, dcnt = {}, {}
        for e in ENGINES:
            c = 0
            for op in self.q[e]:
                if op.dma_key is not None:
                    k = op.dma_key
                    if k not in dsem:
                        dsem[k] = st.enter_context(nc.semaphore("dsem_%s" % (k,)))
                        dcnt[k] = 0
                    dcnt[k] += 16
                    op.sem, op.cnt = dsem[k], dcnt[k]
                elif op.sig:
                    c += 1
                    op.sem, op.cnt = esem[e], c
        block = st.enter_context(nc.Block())
        prog = self

        def run(e, eng_obj):
            waited = {}
            for op in prog.q[e]:
                need = {}
                for d in op.deps:
                    if d.eng == e and d.dma_key is None:
                        if e == "pe" or op.idx - d.idx >= self_gap:
                            continue
                    key = id(d.sem)
                    if key not in need or need[key][1] < d.cnt:
                        need[key] = (d.sem, d.cnt)
                for key, (sem, cnt) in need.items():
                    if waited.get(key, 0) >= cnt:
                        continue
                    eng_obj.wait_ge(sem, cnt)
                    waited[key] = cnt
                ins = op.fn(eng_obj)
                if op.sig:
                    ins.then_inc(op.sem, 16 if op.dma_key is not None else 1)

        @block.tensor
        def _(eng):
            run("pe", eng)

        @block.scalar
        def _(eng):
            run("act", eng)

        @block.vector
        def _(eng):
            run("dve", eng)

        @block.gpsimd
        def _(eng):
            run("pool", eng)

        @block.sync
        def _(eng):
            run("sp", eng)

    def close(self):
        self.stack.close()


CH_KA = 0
CH_KB = 1
CH_VA = 5
CH_VB = 6
CH_QA = 10
CH_QB = 14
CH_GA = 18
CH_GB = 22
N_CH = 26


def build_program(SEQS=SEQS, dbg=None):
    TOK = sum(SEQS)
    nc = bass.Bass("TRN2", target_bir_lowering=False)
    xs = nc.dram_tensor("xs", [TOK, D_MODEL], F32, kind="ExternalInput").ap()
    wch = nc.dram_tensor("wch", [N_CH, 128, 8, 128], F32, kind="ExternalInput").ap()
    wout_d = nc.dram_tensor("wout", [128, 8, 1024], F32, kind="ExternalInput").ap()
    vecs_d = nc.dram_tensor("vecs", [128, 16], F32, kind="ExternalInput").ap()
    lamv_d = nc.dram_tensor("lamv", [128, 4, 64], F32, kind="ExternalInput").ap()
    gfin_d = nc.dram_tensor("gfin", [128, 1024], F32, kind="ExternalInput").ap()
    rope_d = nc.dram_tensor("rope", [4, 128, SMAX], F32, kind="ExternalInput").ap()
    cmat_d = nc.dram_tensor("cmat", [128, 5, 128], F32, kind="ExternalInput").ap()
    ys = nc.dram_tensor("ys", [TOK, D_MODEL], F32, kind="ExternalOutput").ap()
    wsc = nc.dram_tensor("wsc", [N_CH, 128, 8, 128], BF16, kind="Internal").ap()

    P = Prog(nc)
    NT = SMAX // 128

    wout = P.sbuf("wout_bf", [128, 8, 1024], BF16)
    cmat = P.sbuf("cmat_bf", [128, 5, 128], BF16)
    vecs = P.sbuf("vecs_sb", [128, 16], F32)
    gfin = P.sbuf("gfin_sb", [128, 1024], F32)
    small = P.sbuf("small", [128, 32], F32)
    kaT = P.sbuf("kaT", [128, SMAX], BF16)
    kbT = P.sbuf("kbT", [128, 4, SMAX], BF16)
    VA = P.sbuf("VA", [128, NT, 2, 128], BF16)
    VB = P.sbuf("VB", [128, NT, 512], BF16)
    NXB = 4
    xbuf = [P.sbuf("xbuf%d" % i, [128, 1024], F32) for i in range(NXB)]
    junk = P.sbuf("junk", [128, 1024], BF16)
    hbf = P.sbuf("hbf", [128, 1024], BF16)
    hT = P.sbuf("hT", [128, 8, QB], BF16)
    ropeb = P.sbuf("ropeb", [128, 4, QB], F32)
    gbc = ropeb[:, 0:2, :].rearrange("p a (b c) -> p (a b) c", c=128)
    qT = P.sbuf("qT", [128, 8, QB], BF16)
    gate = P.sbuf("gate", [128, 8, QB], BF16)
    mixT = P.sbuf("mixT", [128, 8, QB], BF16)
    NE = 4
    Eb = [P.sbuf("E%d" % i, [128, 1024], BF16) for i in range(NE)]
    NWS = NXB
    wst = [xb_[:].rearrange("p (a b) -> p a b", a=8) for xb_ in xbuf]
    NWB = 4
    wbf = [P.sbuf("wbf%d" % i, [128, 8, 128], BF16) for i in range(NWB)]
    NTMP = 2
    t_sq = [P.sbuf("t_sq%d" % i, [128, QB], BF16) for i in range(NTMP)]
    t_qg = [P.sbuf("t_qg%d" % i, [128, QB], BF16) for i in range(NTMP)]
    t_a = [P.sbuf("t_a%d" % i, [128, QB], F32) for i in range(NTMP)]
    t_b = [P.sbuf("t_b%d" % i, [128, QB], F32) for i in range(NTMP)]
    t_c = [P.sbuf("t_c0", [128, QB], F32)]
    t_d = P.sbuf("t_d", [128, QB], F32)
    t_on, t_r, t_ob, t_obsq = t_a, t_b, t_c[0], t_sq[0]
    ps = P.psum("ps", [128, 4096], F32)

    IDENT, ONES, BLK, RMA, RMB = 0, 1, 2, 3, 4
    C_EPS, C_LAM, C_NLAM, C_CSUB, C_SS, C_LN, C_RSTD, C_E1, C_E2, C_S1, C_S2 = range(11)
    V_GQ, V_GK, V_SUB = 8, 9, 10

    B = {}

    def buf(name):
        if name not in B:
            B[name] = Buf(name)
        return B[name]

    bank = [buf("bank%d" % i) for i in range(8)]
    for bb_ in bank:
        bb_.excl = True

    def bk(i):
        return ps[:, i * 512:(i + 1) * 512]

    state = {"bank_rr": 0, "x_rr": 0, "w_rr": 0, "ws_rr": 0, "tmp_rr": 0, "e_rr": 0}

    def next_bank(n=1):
        i = state["bank_rr"]
        if n == 2 and i % 2 == 1:
            i = (i + 1) % 8
        state["bank_rr"] = (i + n) % 8
        return i

    xb0 = xbuf[0]
    lamt = xbuf[2][:, 0:256].rearrange("p (a b) -> p a b", a=4)
    P.dma("sp", xb0[:, 0:640], cmat_d.rearrange("p a b -> p (a b)"), key="x0", writes=[buf("x0")])
    P.op("dve", lambda e: e.tensor_copy(out=cmat[:].rearrange("p a b -> p (a b)"), in_=xb0[:, 0:640]),
         reads=[buf("x0")], writes=[buf("cmat")])
    P.dma("sp", vecs[:], vecs_d, key="vecs", writes=[buf("vecs")])
    P.dma("sp", lamt, lamv_d, key="x2", writes=[buf("x2")])
    P.dma("sp", gfin[:], gfin_d, key="gfin", writes=[buf("gfin")])
    P.op("pool", lambda e: e.memset(small[:], 0.0), writes=[buf("small"), buf("ss"), buf("rstd"), buf("ss0"), buf("ss1"), buf("rstd0"), buf("rstd1")] + [buf("ess%d" % t_) for t_ in range(4)] + [buf("erstd%d" % t_) for t_ in range(4)])
    P.op("pool", lambda e: e.memset(small[:, C_EPS:C_EPS + 1], EPS), reads=[], writes=[buf("small")])
    P.op("pool", lambda e: e.memset(VA[:].rearrange("p a b c -> p (a b c)"), 1.0), writes=[buf("VA")])
    for k in range(8):
        P.op("dve", lambda e, k=k: e.tensor_scalar(out=gbc[:, k, :], in0=cmat[:, ONES, :],
                                                   scalar1=vecs[:, k:k + 1], scalar2=None, op0=ALU.mult),
             reads=[buf("cmat"), buf("vecs")], writes=[buf("gbc")])
    P.op("dve", lambda e: e.tensor_scalar(out=small[:, C_CSUB:C_CSUB + 1], in0=vecs[:, V_SUB:V_SUB + 1],
                                          scalar1=float(1.0 - LAM_INIT), scalar2=None, op0=ALU.mult),
         reads=[buf("vecs"), buf("small")], writes=[buf("small")])
    lam_tmp = xbuf[1]
    P.op("dve", lambda e: e.tensor_tensor(out=lam_tmp[:, 0:64], in0=lamt[:, 0, :], in1=lamt[:, 1, :], op=ALU.mult),
         reads=[buf("x2")], writes=[buf("x1")])
    P.op("dve", lambda e: e.tensor_tensor(out=lam_tmp[:, 64:128], in0=lamt[:, 2, :], in1=lamt[:, 3, :], op=ALU.mult),
         reads=[buf("x2")], writes=[buf("x1")])
    P.op("dve", lambda e: e.reduce_sum(out=small[:, C_S1:C_S1 + 1], in_=lam_tmp[:, 0:64], axis=mybir.AxisListType.X),
         reads=[buf("x1"), buf("small")], writes=[buf("small")])
    P.op("dve", lambda e: e.reduce_sum(out=small[:, C_S2:C_S2 + 1], in_=lam_tmp[:, 64:128], axis=mybir.AxisListType.X),
         reads=[buf("x1"), buf("small")], writes=[buf("small")])
    P.op("act", lambda e: e.activation(out=small[:, C_E1:C_E1 + 2], in_=small[:, C_S1:C_S1 + 2], func=AF.Exp),
         reads=[buf("small")], writes=[buf("small")])
    P.op("dve", lambda e: e.tensor_tensor(out=small[:, C_LAM:C_LAM + 1], in0=small[:, C_E1:C_E1 + 1],
                                          in1=small[:, C_E2:C_E2 + 1], op=ALU.subtract),
         reads=[buf("small")], writes=[buf("small")])
    P.op("dve", lambda e: e.tensor_scalar(out=small[:, C_LAM:C_LAM + 1], in0=small[:, C_LAM:C_LAM + 1],
                                          scalar1=float(LAM_INIT), scalar2=None, op0=ALU.add),
         reads=[buf("small")], writes=[buf("small")])
    P.op("dve", lambda e: e.tensor_scalar(out=small[:, C_NLAM:C_NLAM + 1], in0=small[:, C_LAM:C_LAM + 1],
                                          scalar1=-1.0, scalar2=None, op0=ALU.mult),
         reads=[buf("small")], writes=[buf("small")])
    for c in range(8):
        xb = xbuf[c % NXB]
        nm = "x%d" % (c % NXB)
        P.dma("sp", xb[:], wout_d[:, c, :], key=nm, writes=[buf(nm)])
        P.op("dve", lambda e, c=c, xb=xb: e.tensor_copy(out=wout[:, c, :], in_=xb[:]),
             reads=[buf(nm)], writes=[buf("wout")])
    state["x_rr"] = 8 % NXB

    prefetched = {}

    def prefetch_x(row_base):
        for tt in range(4):
            r = row_base + tt * 128
            if r not in prefetched:
                prefetched[r] = load_x(r, use_prefetched=False)

    def load_x(row0, use_prefetched=True):
        if use_prefetched and row0 in prefetched:
            return prefetched.pop(row0)
        i = state["x_rr"]
        state["x_rr"] = (i + 1) % NXB
        nm = "x%d" % i
        P.dma("sp", xbuf[i][:], xs[row0:row0 + 128, :], key=nm, writes=[buf(nm)])
        return xbuf[i], buf(nm)

    for ch in range(N_CH):
        si = ch % NWS
        wi = ch % NWB
        sn, wn = "x%d" % si, "wbf%d" % wi
        P.dma("sp", wst[si], wch[ch], key=sn, writes=[buf(sn)])
        P.op("dve" if ch % 3 else "pool", lambda e, si=si, wi=wi: e.tensor_tensor(
            out=wbf[wi][:].rearrange("p a b -> p (a b)"),
            in0=xbuf[si][:],
            in1=ropeb[:, 0:2, :].rearrange("p a b -> p (a b)"), op=ALU.mult),
            reads=[buf(sn), buf("gbc")], writes=[buf(wn)])
        P.dma("act", wsc[ch], wbf[wi][:], key="wsc_%s" % wn, reads=[buf(wn)], writes=[buf("wsc%d" % ch)])

    resident_w = {}

    def load_w(ch):
        if ch in resident_w:
            return resident_w[ch]
        wi = state["w_rr"]
        state["w_rr"] = (wi + 1) % NWB
        wn = "wbf%d" % wi
        P.dma("sp", wbf[wi][:], wsc[ch], key=wn, reads=[buf("wsc%d" % ch)], writes=[buf(wn)])
        return wbf[wi], buf(wn)

    def rstd_from(ms_ap, out_ap, scale, reads, writes, n_part=128):
        P.op("act", lambda e: e.activation(out=out_ap, in_=ms_ap, func=AF.Ln,
                                           bias=small[0:n_part, C_EPS:C_EPS + 1], scale=float(scale)),
             reads=list(reads) + [buf("small")], writes=writes)
        P.op("act", lambda e: e.activation(out=out_ap, in_=out_ap, func=AF.Exp, scale=-0.5),
             reads=writes, writes=writes)

    hbfs = [hbf, junk]
    C_SS4, C_RS4 = 11, 12

    def norm_transpose_block(row_base):
        xs_ = []
        banks_ = []

        def st_stats(tt):
            xb, xB = load_x(row_base + tt * 128)
            xs_.append((xb, xB))
            hb, hB = hbfs[tt % 2], buf("hbf%d" % (tt % 2))
            cs, cr = 11 + (tt % 2), 13 + (tt % 2)
            ssB, rsB = buf("ss%d" % (tt % 2)), buf("rstd%d" % (tt % 2))
            P.op("act", lambda e: e.activation(out=hb[:], in_=xb[:], func=AF.Square,
                                               accum_out=small[:, cs:cs + 1]),
                 reads=[xB], writes=[hB, ssB])
            rstd_from(small[:, cs:cs + 1], small[:, cr:cr + 1], 1.0 / D_MODEL, reads=[ssB], writes=[rsB])

        def st_scale(tt):
            xb, xB = xs_[tt]
            hb, hB = hbfs[tt % 2], buf("hbf%d" % (tt % 2))
            cr = 13 + (tt % 2)
            P.op("dve", lambda e: e.tensor_scalar(out=hb[:], in0=xb[:], scalar1=small[:, cr:cr + 1],
                                                  scalar2=None, op0=ALU.mult),
                 reads=[xB, buf("rstd%d" % (tt % 2))], writes=[hB])

        def st_tr(tt):
            hb, hB = hbfs[tt % 2], buf("hbf%d" % (tt % 2))
            b0 = next_bank(2)
            banks_.append(b0)
            for k in range(8):
                P.op("pe", lambda e, k=k: e.matmul(ps[:, b0 * 512 + k * 128: b0 * 512 + (k + 1) * 128],
                                                   lhsT=hb[:, k * 128:(k + 1) * 128], rhs=cmat[:, IDENT, :],
                                                   start=True, stop=True),
                     reads=[hB, buf("cmat")], writes=[bank[b0 + k // 4]])

        def st_evac(tt):
            b0 = banks_[tt]
            P.op("dve", lambda e: e.tensor_copy(
                out=hT[:, :, tt * 128:(tt + 1) * 128],
                in_=ps[:, b0 * 512:(b0 + 2) * 512].rearrange("p (k t) -> p k t", k=8)),
                reads=[bank[b0], bank[b0 + 1]], writes=[buf("hT")])

        st_stats(0)
        st_stats(1)
        st_scale(0)
        st_tr(0)
        st_scale(1)
        st_tr(1)
        st_stats(2)
        st_evac(0)
        st_stats(3)
        st_scale(2)
        st_tr(2)
        st_evac(1)
        st_scale(3)
        st_tr(3)
        st_evac(2)
        st_evac(3)

    def load_rope(t0):
        P.dma("sp", ropeb[:], rope_d[:, :, t0:t0 + QB].rearrange("f p t -> p f t"), key="rope",
              writes=[buf("rope"), buf("gbc")])

    def proj_feature(wt, wB):
        b = next_bank()
        for k in range(8):
            P.op("pe", lambda e, k=k, b=b: e.matmul(bk(b), lhsT=wt[:, k, :], rhs=hT[:, k, :],
                                                    start=(k == 0), stop=(k == 7)),
                 reads=(wB if isinstance(wB, list) else [wB]) + [buf("hT")], writes=[bank[b]])
        return b

    def rope_chunk(b, is_a, gcol, out_ap, outB):
        i = state["tmp_rr"]
        state["tmp_rr"] = (i + 1) % NTMP
        sq, qg, ta, tb, tc = t_sq[i], t_qg[i], t_a[i], t_b[i], t_c[0]
        sqB, qgB, taB, tbB, tcB = (buf("sq%d" % i), buf("qg%d" % i), buf("ta%d" % i),
                                   buf("tb%d" % i), buf("tc0"))
        ci, si = (0, 1) if is_a else (2, 3)
        rm = RMA if is_a else RMB
        bm = None
        if is_a:
            P.op("act", lambda e: e.activation(out=sq[:], in_=bk(b), func=AF.Square),
                 reads=[bank[b]], writes=[sqB])
            bm = next_bank()
            P.op("pe", lambda e: e.matmul(bk(bm), lhsT=cmat[:, BLK, :], rhs=sq[:], start=True, stop=True),
                 reads=[sqB, buf("cmat")], writes=[bank[bm]])
            P.op("act", lambda e: e.activation(out=qg[:], in_=bk(b), func=AF.Copy,
                                               scale=vecs[:, gcol:gcol + 1]),
                 reads=[bank[b], buf("vecs")], writes=[qgB])
            P.op("dve", lambda e: e.scalar_tensor_tensor(out=ta[:], in0=bk(b), scalar=vecs[:, gcol:gcol + 1],
                                                         in1=ropeb[:, ci, :], op0=ALU.mult, op1=ALU.mult),
                 reads=[bank[b], buf("vecs"), buf("rope")], writes=[taB])
        else:
            P.op("act", lambda e: e.activation(out=qg[:], in_=bk(b), func=AF.Copy), reads=[bank[b]], writes=[qgB])
            P.op("dve", lambda e: e.tensor_tensor(out=ta[:], in0=bk(b), in1=ropeb[:, ci, :], op=ALU.mult),
                 reads=[bank[b], buf("rope")], writes=[taB])
        br = next_bank()
        P.op("pe", lambda e: e.matmul(bk(br), lhsT=cmat[:, rm, :], rhs=qg[:], start=True, stop=True),
             reads=[qgB, buf("cmat")], writes=[bank[br]])

        def stage2():
            if is_a:
                rstd_from(bk(bm), tc[:], 1.0 / HEAD_DIM, reads=[bank[bm]], writes=[tcB])
            P.op("dve", lambda e: e.tensor_tensor(out=tb[:], in0=bk(br), in1=ropeb[:, si, :], op=ALU.mult),
                 reads=[bank[br], buf("rope")], writes=[tbB])
            if is_a:
                P.op("dve", lambda e: e.tensor_tensor(out=ta[:], in0=ta[:], in1=tb[:], op=ALU.add),
                     reads=[taB, tbB], writes=[taB])
                P.op("dve", lambda e: e.tensor_tensor(out=out_ap, in0=ta[:], in1=tc[:], op=ALU.mult),
                     reads=[taB, tcB], writes=[outB])
            else:
                P.op("dve", lambda e: e.tensor_tensor(out=out_ap, in0=ta[:], in1=tb[:], op=ALU.add),
                     reads=[taB, tbB], writes=[outB])
        return stage2

    def gate_chunk(b, c):
        i = state["tmp_rr"]
        state["tmp_rr"] = (i + 1) % NTMP
        ta = t_a[i]
        taB = buf("ta%d" % i)
        P.op("act", lambda e: e.activation(out=ta[:], in_=bk(b), func=AF.Tanh, scale=0.5),
             reads=[bank[b]], writes=[taB])
        P.op("dve", lambda e: e.scalar_tensor_tensor(out=gate[:, c, :], in0=ta[:], scalar=1.0, in1=bk(b),
                                                     op0=ALU.add, op1=ALU.mult),
             reads=[bank[b], taB], writes=[buf("gate")])

    def run_pipelined(tasks):
        prev = None
        pending2 = None
        for ch, proj, post in tasks:
            wt, wB = load_w(ch)
            b = proj(wt, wB)
            if prev is not None:
                p2 = prev[1](prev[0])
                if pending2 is not None:
                    pending2()
                pending2 = p2
            prev = (b, post)
        p2 = prev[1](prev[0])
        if pending2 is not None:
            pending2()
        if p2 is not None:
            p2()

    def do_sequence(base, S):
        nblk = S // QB
        nkb = S // 128
        if dbg == 0:
            return
        kslots = []
        for j in range(4):
            kslots.append((qT[:, :, j * 128:(j + 1) * 128], [buf("qT")]))
        for j in range(4):
            kslots.append((mixT[:, :, j * 128:(j + 1) * 128], [buf("mixT%d" % c_) for c_ in range(8)]))
        for j in range(2):
            kslots.append((gate[:, :, j * 128:(j + 1) * 128], [buf("gate")]))
        kchunks = [CH_KA] + [CH_KB + i for i in range(4)] + [CH_VA] + [CH_VB + i for i in range(4)]
        for i, ch in enumerate(kchunks):
            view, parents = kslots[i]
            kb_ = buf("kw%d" % i)
            prev_users = []
            for pb in parents:
                prev_users.extend(pb.readers())
                if pb.w is not None:
                    prev_users.append(pb.w)
            P.dma("sp", view, wsc[ch], key="kw%d" % i, reads=[buf("wsc%d" % ch)], writes=[kb_],
                  extra=prev_users)
            resident_w[ch] = (view, [kb_] + parents)
        for blk in range(nblk):
            t0 = blk * QB
            load_rope(t0)
            if dbg == 0.1:
                return
            norm_transpose_block(base + t0)
            if blk + 1 < nblk:
                prefetch_x(base + t0 + QB)
            else:
                prefetch_x(base)
            if dbg == 0.3:
                return
            tl0 = t0 // 128

            def v_proj(wt, wB):
                b = next_bank()
                for tt in range(4):
                    for k in range(8):
                        P.op("pe", lambda e, k=k, tt=tt: e.matmul(
                            ps[:, b * 512 + tt * 128: b * 512 + (tt + 1) * 128],
                            lhsT=hT[:, k, tt * 128:(tt + 1) * 128], rhs=wt[:, k, :],
                            start=(k == 0), stop=(k == 7)),
                            reads=(wB if isinstance(wB, list) else [wB]) + [buf("hT")], writes=[bank[b]])
                return b

            def va_evac(b):
                P.op("act", lambda e, tl0=tl0: e.activation(
                    out=VA[:, tl0:tl0 + 4, :, 0:64],
                    in_=bk(b).rearrange("p (t h d) -> p t h d", t=4, h=2), func=AF.Copy),
                    reads=[bank[b]], writes=[buf("VA")])

            def vb_evac(b, hb):
                P.op("act", lambda e, tl0=tl0: e.activation(
                    out=VB[:, tl0:tl0 + 4, hb * 128:(hb + 1) * 128],
                    in_=bk(b).rearrange("p (t c) -> p t c", t=4), func=AF.Copy),
                    reads=[bank[b]], writes=[buf("VB")])

            tasks = [(CH_KA, proj_feature, lambda b: rope_chunk(b, True, V_GK, kaT[:, t0:t0 + QB], buf("kaT")))]
            for i in range(4):
                tasks.append((CH_KB + i, proj_feature,
                              lambda b, i=i: rope_chunk(b, False, None, kbT[:, i, t0:t0 + QB], buf("kbT"))))
            tasks.append((CH_VA, v_proj, va_evac))
            for i in range(4):
                tasks.append((CH_VB + i, v_proj, lambda b, i=i: vb_evac(b, i)))
            run_pipelined(tasks)

        resident_w.clear()
        if dbg == 1:
            return
        def q_proj(blk):
            t0 = blk * QB
            load_rope(t0)
            norm_transpose_block(base + t0)
            tasks = []
            for j in range(4):
                tasks.append((CH_QA + j, proj_feature,
                              lambda b, j=j: rope_chunk(b, True, V_GQ, qT[:, j, :], buf("qT"))))
            for j in range(4):
                tasks.append((CH_QB + j, proj_feature,
                              lambda b, j=j: rope_chunk(b, False, None, qT[:, 4 + j, :], buf("qT"))))
            for j in range(8):
                tasks.append((CH_GA + j, proj_feature, lambda b, j=j: gate_chunk(b, j)))
            run_pipelined(tasks)

        q_proj(0)
        for blk in range(nblk):
            t0 = blk * QB
            if dbg == 2:
                return
            if blk + 1 < nblk:
                prefetch_x(base + t0 + QB)
            pairs = []
            for i in range(4):
                pairs.append((1, i))
                pairs.append((2, i))
            iters = [(pi, kb) for pi in range(len(pairs)) for kb in range(nkb)]
            st_tiles = [(0, 1), (2, 3)]
            deferred = {}
            BANK_O, BANK_S = 5, 6

            def a_bank(pi):
                return 4 if pi % 2 == 0 else 7

            def ksl(kb):
                return slice(kb * 128, (kb + 1) * 128)

            def halves(pi):
                typ, i = pairs[pi]
                return [("A", 0), ("B", 1)] if typ == 1 else [("B", 0), ("A", 1)]

            def emit_qk(n):
                pi, kb = iters[n]
                typ, i = pairs[pi]
                for kind, u in halves(pi):
                    bb = st_tiles[n % 2][u]
                    p0 = u * 64
                    if kind == "A":
                        P.op("pe", lambda e, bb=bb, p0=p0: e.matmul(
                            bk(bb), lhsT=kaT[p0:p0 + 64, ksl(kb)], rhs=qT[p0:p0 + 64, i, :],
                            start=True, stop=True),
                            reads=[buf("kaT"), buf("qT")], writes=[bank[bb]])
                    else:
                        P.op("pe", lambda e, bb=bb, p0=p0: e.matmul(
                            bk(bb), lhsT=kbT[p0:p0 + 64, i, ksl(kb)], rhs=qT[p0:p0 + 64, 4 + i, :],
                            start=True, stop=True),
                            reads=[buf("kbT"), buf("qT")], writes=[bank[bb]])

            def emit_exp(n):
                b0, b1 = st_tiles[n % 2]
                ei = state["e_rr"]
                state["e_rr"] = (ei + 1) % NE
                E, EB = Eb[ei], buf("E%d" % ei)
                P.op("act", lambda e: e.activation(out=E[:], in_=ps[:, b0 * 512:(b0 + 2) * 512],
                                                   func=AF.Exp, scale=0.125),
                     reads=[bank[b0], bank[b1]], writes=[EB])
                return E, EB

            def emit_pv(n, E, EB):
                pi, kb = iters[n]
                typ, i = pairs[pi]
                first = (kb == 0)
                last = (kb == nkb - 1)
                ab = a_bank(pi)
                for kind, u in halves(pi):
                    if kind == "A":
                        P.op("pe", lambda e, u=u: e.matmul(
                            bk(ab), lhsT=VA[:, kb, u, :], rhs=E[:, u * 512:(u + 1) * 512],
                            start=first, stop=last),
                            reads=[EB, buf("VA")], writes=[bank[ab]])
                    else:
                        P.op("pe", lambda e, u=u: e.matmul(
                            bk(BANK_O), lhsT=VB[:, kb, i * 128:(i + 1) * 128], rhs=E[:, u * 512:(u + 1) * 512],
                            start=first, stop=last),
                            reads=[EB, buf("VB")], writes=[bank[BANK_O]])
                        P.op("pe", lambda e, u=u: e.matmul(
                            bk(BANK_S), lhsT=cmat[:, ONES, :], rhs=E[:, u * 512:(u + 1) * 512],
                            start=first, stop=last),
                            reads=[EB, buf("cmat")], writes=[bank[BANK_S]])

            def finalize_pair(pi):
                typ, i = pairs[pi]
                ab = a_bank(pi)
                tO, tS = t_a[0], t_b[0]
                P.op("dve", lambda e: e.tensor_copy(out=tO[:], in_=bk(BANK_O)),
                     reads=[bank[BANK_O]], writes=[buf("ta0")])
                P.op("dve", lambda e: e.tensor_copy(out=tS[:], in_=bk(BANK_S)),
                     reads=[bank[BANK_S]], writes=[buf("tb0")])
                o_sb, s_sb = t_b[1], t_d
                P.op("dve", lambda e: e.tensor_copy(out=o_sb[0:64, :], in_=ps[0:64, ab * 512:(ab + 1) * 512]),
                     reads=[bank[ab]], writes=[buf("tb1")])
                P.op("dve", lambda e: e.tensor_copy(out=s_sb[0:64, :], in_=ps[64:128, ab * 512:(ab + 1) * 512]),
                     reads=[bank[ab]], writes=[buf("td")])
                P.op("dve", lambda e: e.reciprocal(out=tS[:], in_=tS[:]), reads=[buf("tb0")], writes=[buf("tb0")])
                if typ == 1:
                    P.op("dve", lambda e: e.tensor_tensor(out=t_a[1][:], in0=tO[:], in1=tS[:], op=ALU.mult),
                         reads=[buf("ta0"), buf("tb0")], writes=[buf("ta1")])
                else:
                    P.op("dve", lambda e: e.tensor_tensor(out=tO[:], in0=tO[:], in1=tS[:], op=ALU.mult),
                         reads=[buf("ta0"), buf("tb0")], writes=[buf("ta0")])
                    P.op("dve", lambda e: e.scalar_tensor_tensor(
                        out=t_ob[:], in0=t_a[1][:], scalar=small[:, C_NLAM:C_NLAM + 1], in1=tO[:],
                        op0=ALU.mult, op1=ALU.add),
                        reads=[buf("ta0"), buf("ta1"), buf("small")], writes=[buf("tc0")])
                    P.op("dve", lambda e: e.tensor_tensor(out=t_obsq[:], in0=t_ob[:], in1=t_ob[:], op=ALU.mult),
                         reads=[buf("tc0")], writes=[buf("sq0")])
                u = 0 if typ == 1 else 1
                p0 = u * 64
                P.op("dve", lambda e: e.reciprocal(out=s_sb[0:64, :], in_=s_sb[0:64, :]),
                     reads=[buf("td")], writes=[buf("td")])
                if p0 == 0:
                    P.op("dve", lambda e: e.tensor_tensor(out=o_sb[0:64, :], in0=o_sb[0:64, :], in1=s_sb[0:64, :],
                                                          op=ALU.mult),
                         reads=[buf("tb1"), buf("td")], writes=[buf("tb1")])
                    src, srcB = o_sb, buf("tb1")
                else:
                    P.op("dve", lambda e: e.tensor_tensor(out=s_sb[64:128, :], in0=o_sb[0:64, :], in1=s_sb[0:64, :],
                                                          op=ALU.mult),
                         reads=[buf("tb1"), buf("td")], writes=[buf("td")])
                    src, srcB = s_sb, buf("td")
                P.op("dve", lambda e: e.scalar_tensor_tensor(
                    out=mixT[p0:p0 + 64, i, :], in0=src[p0:p0 + 64, :], scalar=0.5, in1=gate[p0:p0 + 64, i, :],
                    op0=ALU.mult, op1=ALU.mult),
                    reads=[srcB, buf("gate")], writes=[buf("mixT%d" % i)])

            def finalize_b2a(i, bm):
                P.op("pe", lambda e: e.matmul(bk(bm), lhsT=cmat[:, ONES, :], rhs=t_obsq[:], start=True, stop=True),
                     reads=[buf("sq0"), buf("cmat")], writes=[bank[bm]])

            def finalize_b2b(i, bm):
                rstd_from(bk(bm), t_b[0][:], 1.0 / 128.0, reads=[bank[bm]], writes=[buf("tb0")])

            def finalize_b2c(i, bm):
                obB = buf("tc0")
                rs = t_b[0]
                P.op("dve", lambda e: e.scalar_tensor_tensor(
                    out=t_ob[:], in0=t_ob[:], scalar=small[:, C_CSUB:C_CSUB + 1], in1=rs[:],
                    op0=ALU.mult, op1=ALU.mult),
                    reads=[obB, buf("tb0"), buf("small")], writes=[obB])
                P.op("dve", lambda e: e.scalar_tensor_tensor(
                    out=mixT[:, 4 + i, :], in0=t_ob[:], scalar=0.5, in1=gate[:, 4 + i, :],
                    op0=ALU.mult, op1=ALU.mult),
                    reads=[obB, buf("gate")], writes=[buf("mixT%d" % (4 + i))])

            nit = len(iters)
            emit_qk(0)
            if nit > 1:
                emit_qk(1)
            for n in range(nit):
                E_, EB_ = emit_exp(n)
                if n + 2 < nit:
                    emit_qk(n + 2)
                emit_pv(n, E_, EB_)
                for fn_ in deferred.pop(n, []):
                    fn_()
                pi, kb = iters[n]
                if kb == nkb - 1:
                    typ, i = pairs[pi]
                    finalize_pair(pi)
                    if typ == 2:
                        bm = a_bank(pi)
                        d0 = max(1, min(10, nkb - 3))
                        for dd, fn2 in ((d0, finalize_b2a), (d0 + 2, finalize_b2b), (d0 + 3, finalize_b2c)):
                            deferred.setdefault(min(n + dd, nit - 1), []).append(
                                lambda i=i, bm=bm, fn2=fn2: fn2(i, bm))
            for k_ in sorted(deferred):
                for fn_ in deferred[k_]:
                    fn_()

            if dbg == 3:
                return
            if blk + 1 < nblk:
                q_proj(blk + 1)
            c_order = [0, 4, 1, 5, 2, 6, 3, 7]
            xts = []
            for tt in range(4):
                xts.append(load_x(base + t0 + tt * 128))
            for ci, c in enumerate(c_order):
                for tt in range(4):
                    for half in range(2):
                        bb = tt * 2 + half
                        P.op("pe", lambda e, c=c, half=half, bb=bb, tt=tt, ci=ci: e.matmul(
                            bk(bb), lhsT=mixT[:, c, tt * 128:(tt + 1) * 128],
                            rhs=wout[:, c, half * 512:(half + 1) * 512], start=(ci == 0), stop=(ci == 7)),
                            reads=[buf("mixT%d" % c), buf("wout")], writes=[bank[bb]])
            def ep_add(tt):
                xb, xB = xts[tt]
                b0 = tt * 2
                hb, hB = hbfs[tt % 2], buf("hbf%d" % (tt % 2))
                cs, cr = 16 + tt, 20 + tt
                ssB, rsB = buf("ess%d" % tt), buf("erstd%d" % tt)
                P.op("dve", lambda e: e.tensor_tensor(out=xb[:], in0=ps[:, b0 * 512:(b0 + 2) * 512],
                                                      in1=xb[:], op=ALU.add),
                     reads=[bank[b0], bank[b0 + 1], xB], writes=[xB])
                P.op("act", lambda e: e.activation(out=hb[:], in_=xb[:], func=AF.Square,
                                                   accum_out=small[:, cs:cs + 1]),
                     reads=[xB], writes=[hB, ssB])
                rstd_from(small[:, cs:cs + 1], small[:, cr:cr + 1], 1.0 / D_MODEL, reads=[ssB], writes=[rsB])

            def ep_scale(tt):
                xb, xB = xts[tt]
                cr = 20 + tt
                row0 = base + t0 + tt * 128
                P.op("dve", lambda e: e.scalar_tensor_tensor(
                    out=xb[:], in0=xb[:], scalar=small[:, cr:cr + 1], in1=gfin[:],
                    op0=ALU.mult, op1=ALU.mult),
                    reads=[xB, buf("erstd%d" % tt), buf("gfin")], writes=[xB])
                store_ops.append(P.dma("pool", ys[row0:row0 + 128, :], xb[:], key="st_" + xB.name, reads=[xB]))

            ep_add(0)
            ep_add(1)
            ep_add(2)
            ep_scale(0)
            ep_add(3)
            ep_scale(1)
            ep_scale(2)
            ep_scale(3)

    store_ops = []
    base = 0
    for S in SEQS:
        do_sequence(base, S)
        base += S
    P.op("sp", lambda e: None, extra=store_ops)
    return nc, P


def _rope_tables():
    f = np.float32
    t = np.arange(SMAX)
    row = (t // GRID_W).astype(f)
    col = (t % GRID_W).astype(f)
    pos = t.astype(f)
    inv16 = (f(ROPE_THETA) ** (-(np.arange(0, 32, 2).astype(f)) / f(32))).astype(f)
    inv32 = (f(ROPE_THETA) ** (-(np.arange(0, 64, 2).astype(f)) / f(64))).astype(f)
    tabs = np.zeros((4, 128, SMAX), f)
    for p in range(128):
        d = p % 64
        i = (d % 32) % 16
        ps_ = row if d < 32 else col
        ang = (ps_ * inv16[i]).astype(f)
        tabs[0, p] = np.cos(ang)
        tabs[1, p] = np.sin(ang)
        i2 = d % 32
        ang2 = (pos * inv32[i2]).astype(f)
        tabs[2, p] = np.cos(ang2)
        tabs[3, p] = np.sin(ang2)
    return tabs


def _const_mats():
    f = np.float32
    cm = np.zeros((128, 5, 128), f)
    cm[:, 0, :] = np.eye(128, dtype=f)
    cm[:, 1, :] = 1.0
    for p in range(128):
        for m in range(128):
            if p // 64 == m // 64:
                cm[p, 2, m] = 1.0
    for m in range(128):
        d = m % 32
        if d < 16:
            cm[m + 16, 3, m] = -1.0
        else:
            cm[m - 16, 3, m] = 1.0
        d2 = m % 64
        if d2 < 32:
            cm[m + 32, 4, m] = -1.0
        else:
            cm[m - 32, 4, m] = 1.0
    return cm


def _pack_weights(w_in, w_out):
    QA0, KA0, VA0, GA0, QB0, KB0, VB0, GB0 = 0, 512, 640, 768, 1280, 1792, 2304, 2816
    cols = []
    cols.append(np.arange(KA0, KA0 + 128))
    for i in range(4):
        cols.append(np.arange(KB0 + i * 128, KB0 + (i + 1) * 128))
    cols.append(np.arange(VA0, VA0 + 128))
    for i in range(4):
        cols.append(np.arange(VB0 + i * 128, VB0 + (i + 1) * 128))
    for j in range(4):
        cols.append(np.concatenate([np.arange(QA0 + j * 64, QA0 + (j + 1) * 64),
                                    np.arange(QA0 + (j + 4) * 64, QA0 + (j + 5) * 64)]))
    for i in range(4):
        cols.append(np.arange(QB0 + i * 128, QB0 + (i + 1) * 128))
    for j in range(4):
        cols.append(np.concatenate([np.arange(GA0 + j * 64, GA0 + (j + 1) * 64),
                                    np.arange(GA0 + (j + 4) * 64, GA0 + (j + 5) * 64)]))
    for i in range(4):
        cols.append(np.arange(GB0 + i * 128, GB0 + (i + 1) * 128))
    assert len(cols) == N_CH
    w = w_in[0]
    wch = np.empty((N_CH, 128, 8, 128), np.float32)
    wr = w.reshape(8, 128, -1)
    for c, cl in enumerate(cols):
        wch[c] = wr[:, :, cl].transpose(1, 0, 2)
    rows = []
    for j in range(4):
        rows.append(np.concatenate([np.arange(j * 64, (j + 1) * 64), np.arange((j + 4) * 64, (j + 5) * 64)]))
    for hb in range(4):
        rows.append(np.arange(512 + hb * 128, 512 + (hb + 1) * 128))
    wo = w_out[0]
    wout = np.empty((128, 8, 1024), np.float32)
    for c, rw in enumerate(rows):
        wout[:, c, :] = wo[rw, :]
    return wch, wout


_CACHE = {}


def kernel(x_prompt, x_sample, g_norm, w_in, a_q_norm, a_k_norm, b_lambda_q1, b_lambda_k1,
           b_lambda_q2, b_lambda_k2, b_subln, w_out, g_final):
    f = np.float32
    x_prompt = np.asarray(x_prompt, f)
    x_sample = np.asarray(x_sample, f)
    wch, wout = _pack_weights(np.asarray(w_in, f), np.asarray(w_out, f))
    vecs = np.zeros((128, 16), f)
    vecs[:, 0:8] = np.asarray(g_norm, f)[0].reshape(8, 128).T
    vecs[:, 8] = np.tile(np.asarray(a_q_norm, f)[0], 2)
    vecs[:, 9] = np.tile(np.asarray(a_k_norm, f)[0], 2)
    vecs[:, 10] = np.asarray(b_subln, f)[0]
    lamv = np.stack([np.asarray(v, f)[0] for v in (b_lambda_q1, b_lambda_k1, b_lambda_q2, b_lambda_k2)], 0)
    lamv = np.ascontiguousarray(np.broadcast_to(lamv[None], (128, 4, 64)))
    gfin = np.ascontiguousarray(np.broadcast_to(np.asarray(g_final, f)[None, :], (128, 1024)))
    if "tabs" not in _CACHE:
        _CACHE["tabs"] = (_rope_tables(), _const_mats())
    rope, cmat = _CACHE["tabs"]

    in_maps = []
    for c in range(N_CORES):
        xs = np.concatenate([x_prompt[2 * c], x_prompt[2 * c + 1], x_sample[c]], axis=0)
        in_maps.append({"xs": np.ascontiguousarray(xs), "wch": wch, "wout": wout, "vecs": vecs,
                        "lamv": lamv, "gfin": gfin, "rope": rope, "cmat": cmat})
    nc, P = build_program()
    P.emit()
    P.close()
    res = run_bass_kernel_spmd(nc, in_maps, core_ids=list(range(N_CORES)))
    y_prompt = np.empty_like(x_prompt)
    y_sample = np.empty_like(x_sample)
    for c in range(N_CORES):
        ys = res.results[c]["ys"]
        y_prompt[2 * c] = ys[0:2048]
        y_prompt[2 * c + 1] = ys[2048:4096]
        y_sample[c] = ys[4096:8192]
    return (y_prompt, y_sample)
```

```python
import contextlib
import numpy as np
import concourse.bass as bass
import concourse.mybir as mybir
from concourse.bass_utils import run_bass_kernel_spmd

F32 = mybir.dt.float32
BF16 = mybir.dt.bfloat16
ALU = mybir.AluOpType
AF = mybir.ActivationFunctionType

D_MODEL = 1024
HEAD_DIM = 64
GRID_W = 64
ROPE_THETA = 10000.0
EPS = 1e-6
N_CORES = 8
SEQS = (2048, 2048, 4096)
TOK = sum(SEQS)
SMAX = 4096
QB = 512
LAM_INIT = 0.8 - 0.6 * 1.0

ENGINES = ("pe", "act", "dve", "pool", "sp")


class Op:
    __slots__ = ("eng", "fn", "deps", "sig", "sem", "cnt", "idx", "dma_key", "raw")

    def __init__(self, eng, fn, deps, dma_key=None):
        self.raw = set(id(d) for d in deps)
        self.eng = eng
        self.fn = fn
        self.deps = deps
        self.sig = False
        self.sem = None
        self.cnt = 0
        self.idx = -1
        self.dma_key = dma_key


class Buf:
    __slots__ = ("name", "w", "r", "rd", "excl")

    def __init__(self, name, excl=False):
        self.name = name
        self.excl = excl
        self.w = None
        self.r = {}
        self.rd = []

    def readers(self):
        return list(self.r.values()) + self.rd


class Prog:
    def __init__(self, nc):
        self.nc = nc
        self.q = {e: [] for e in ENGINES}
        self.stack = contextlib.ExitStack()

    def sbuf(self, name, shape, dtype):
        return self.stack.enter_context(self.nc.sbuf_tensor(name, list(shape), dtype))

    def psum(self, name, shape, dtype):
        return self.stack.enter_context(self.nc.psum_tensor(name, list(shape), dtype))

    def _record(self, op, reads, writes):
        deps = op.deps
        raw = op.raw
        for b in reads:
            if b.w is not None:
                deps.append(b.w)
                raw.add(id(b.w))
            if b.excl:
                for en, r in b.r.items():
                    if en != op.eng:
                        deps.append(r)
        for b in writes:
            deps.extend(b.readers())
            if b.w is not None:
                deps.append(b.w)
        op.idx = len(self.q[op.eng])
        self.q[op.eng].append(op)
        for b in reads:
            if op.dma_key is not None:
                b.rd.append(op)
            else:
                b.r[op.eng] = op
        for b in writes:
            b.w = op
            b.r = {}
            b.rd = []
        return op

    def op(self, eng, fn, reads=(), writes=(), extra=()):
        return self._record(Op(eng, fn, [d for d in extra if d is not None]), reads, writes)

    def dma(self, eng, out, in_, key, reads=(), writes=(), extra=()):
        def fn(e, out=out, in_=in_):
            return e.dma_start(out=out, in_=in_)
        return self._record(Op(eng, fn, [d for d in extra if d is not None], dma_key=key),
                            reads, writes)

    def emit(self, self_gap=1 << 30):
        nc = self.nc
        st = self.stack
        INORDER = ("pe", "act", "dve")

        def eff_deps(op, e):
            best = {}
            rest = []
            for d in op.deps:
                if d.dma_key is None and d.eng in INORDER:
                    if d.eng == e and e == "pe":
                        continue
                    if d.eng not in best or best[d.eng].idx < d.idx:
                        best[d.eng] = d
                else:
                    rest.append(d)
            return list(best.values()) + rest

        for e in ENGINES:
            for op in self.q[e]:
                op.deps = eff_deps(op, e)
                for d in op.deps:
                    d.sig = True
                if op.dma_key is not None:
                    op.sig = True
        esem = {e: st.enter_context(nc.semaphore("sem_" + e)) for e in ENGINES}
        dsem, dcnt = {}, {}
        for e in ENGINES:
            c = 0
            for op in self.q[e]:
                if op.dma_key is not None:
                    k = op.dma_key
                    if k not in dsem:
                        dsem[k] = st.enter_context(nc.semaphore("dsem_%s" % (k,)))
                        dcnt[k] = 0
                    dcnt[k] += 16
                    op.sem, op.cnt = dsem[k], dcnt[k]
                elif op.sig:
                    c += 1
                    op.sem, op.cnt = esem[e], c
        block = st.enter_context(nc.Block())
        prog = self

        def run(e, eng_obj):
            waited = {}
            for op in prog.q[e]:
                need = {}
                for d in op.deps:
                    key = id(d.sem)
                    if key not in need or need[key][1] < d.cnt:
                        need[key] = (d.sem, d.cnt)
                for key, (sem, cnt) in need.items():
                    if waited.get(key, 0) >= cnt:
                        continue
                    eng_obj.wait_ge(sem, cnt)
                    waited[key] = cnt
                ins = op.fn(eng_obj)
                if op.sig:
                    ins.then_inc(op.sem, 16 if op.dma_key is not None else 1)

        @block.tensor
        def _(eng):
            run("pe", eng)

        @block.scalar
        def _(eng):
            run("act", eng)

        @block.vector
        def _(eng):
            run("dve", eng)

        @block.gpsimd
        def _(eng):
            run("pool", eng)

        @block.sync
        def _(eng):
            run("sp", eng)

    def close(self):
        self.stack.close()


CH_KA = 0
CH_KB = 1
CH_VA = 5
CH_VB = 6
CH_QA = 10
CH_QB = 14
CH_GA = 18
CH_GB = 22
N_CH = 26


def build_program(SEQS=SEQS, dbg=None):
    TOK = sum(SEQS)
    nc = bass.Bass("TRN2", target_bir_lowering=False)
    xs = nc.dram_tensor("xs", [TOK, D_MODEL], F32, kind="ExternalInput").ap()
    wch = nc.dram_tensor("wch", [N_CH, 128, 8, 128], F32, kind="ExternalInput").ap()
    wout_d = nc.dram_tensor("wout", [128, 8, 1024], F32, kind="ExternalInput").ap()
    vecs_d = nc.dram_tensor("vecs", [128, 16], F32, kind="ExternalInput").ap()
    lamv_d = nc.dram_tensor("lamv", [128, 4, 64], F32, kind="ExternalInput").ap()
    gfin_d = nc.dram_tensor("gfin", [128, 1024], F32, kind="ExternalInput").ap()
    rope_d = nc.dram_tensor("rope", [4, 128, SMAX], F32, kind="ExternalInput").ap()
    cmat_d = nc.dram_tensor("cmat", [128, 5, 128], F32, kind="ExternalInput").ap()
    ys = nc.dram_tensor("ys", [TOK, D_MODEL], F32, kind="ExternalOutput").ap()
    wsc = nc.dram_tensor("wsc", [N_CH, 128, 8, 128], BF16, kind="Internal").ap()

    P = Prog(nc)
    NT = SMAX // 128

    wout = P.sbuf("wout_bf", [128, 8, 1024], BF16)
    cmat = P.sbuf("cmat_bf", [128, 5, 128], BF16)
    vecs = P.sbuf("vecs_sb", [128, 16], F32)
    gfin = P.sbuf("gfin_sb", [128, 1024], F32)
    small = P.sbuf("small", [128, 32], F32)
    kaT = P.sbuf("kaT", [128, SMAX], BF16)
    kbT = P.sbuf("kbT", [128, 4, SMAX], BF16)
    VA = P.sbuf("VA", [128, NT, 2, 128], BF16)
    VB = P.sbuf("VB", [128, NT, 512], BF16)
    NXB = 4
    xbuf = [P.sbuf("xbuf%d" % i, [128, 1024], F32) for i in range(NXB)]
    junk = P.sbuf("junk", [128, 1024], BF16)
    hbf = P.sbuf("hbf", [128, 1024], BF16)
    hT = P.sbuf("hT", [128, 8, QB], BF16)
    ropeb = P.sbuf("ropeb", [128, 4, QB], F32)
    gbc = ropeb[:, 0:2, :].rearrange("p a (b c) -> p (a b) c", c=128)
    qT = P.sbuf("qT", [128, 8, QB], BF16)
    gate = P.sbuf("gate", [128, 8, QB], BF16)
    mixT = P.sbuf("mixT", [128, 8, QB], BF16)
    NE = 4
    Eb = [P.sbuf("E%d" % i, [128, 1024], BF16) for i in range(NE)]
    NWS = NXB
    wst = [xb_[:].rearrange("p (a b) -> p a b", a=8) for xb_ in xbuf]
    NWB = 4
    wbf = [P.sbuf("wbf%d" % i, [128, 8, 128], BF16) for i in range(NWB)]
    NTMP = 2
    t_sq = [P.sbuf("t_sq%d" % i, [128, QB], BF16) for i in range(NTMP)]
    t_qg = [P.sbuf("t_qg%d" % i, [128, QB], BF16) for i in range(NTMP)]
    t_a = [P.sbuf("t_a%d" % i, [128, QB], F32) for i in range(NTMP)]
    t_b = [P.sbuf("t_b%d" % i, [128, QB], F32) for i in range(NTMP)]
    t_c = [P.sbuf("t_c0", [128, QB], F32)]
    t_d = P.sbuf("t_d", [128, QB], F32)
    t_on, t_r, t_ob, t_obsq = t_a, t_b, t_c[0], t_sq[0]
    ps = P.psum("ps", [128, 4096], F32)

    IDENT, ONES, BLK, RMA, RMB = 0, 1, 2, 3, 4
    C_EPS, C_LAM, C_NLAM, C_CSUB, C_SS, C_LN, C_RSTD, C_E1, C_E2, C_S1, C_S2 = range(11)
    V_GQ, V_GK, V_SUB = 8, 9, 10

    B = {}

    def buf(name):
        if name not in B:
            B[name] = Buf(name)
        return B[name]

    bank = [buf("bank%d" % i) for i in range(8)]
    for bb_ in bank:
        bb_.excl = True

    def bk(i):
        return ps[:, i * 512:(i + 1) * 512]

    state = {"bank_rr": 0, "x_rr": 0, "w_rr": 0, "ws_rr": 0, "tmp_rr": 0, "e_rr": 0}

    def next_bank(n=1):
        i = state["bank_rr"]
        if n == 2 and i % 2 == 1:
            i = (i + 1) % 8
        state["bank_rr"] = (i + n) % 8
        return i

    xb0 = xbuf[0]
    lamt = xbuf[2][:, 0:256].rearrange("p (a b) -> p a b", a=4)
    P.dma("sp", xb0[:, 0:640], cmat_d.rearrange("p a b -> p (a b)"), key="x0", writes=[buf("x0")])
    P.op("dve", lambda e: e.tensor_copy(out=cmat[:].rearrange("p a b -> p (a b)"), in_=xb0[:, 0:640]),
         reads=[buf("x0")], writes=[buf("cmat")])
    P.dma("sp", vecs[:], vecs_d, key="vecs", writes=[buf("vecs")])
    P.dma("sp", lamt, lamv_d, key="x2", writes=[buf("x2")])
    P.dma("sp", gfin[:], gfin_d, key="gfin", writes=[buf("gfin")])
    P.op("pool", lambda e: e.memset(small[:], 0.0), writes=[buf("small"), buf("ss"), buf("rstd"), buf("ss0"), buf("ss1"), buf("rstd0"), buf("rstd1")] + [buf("ess%d" % t_) for t_ in range(4)] + [buf("erstd%d" % t_) for t_ in range(4)])
    P.op("pool", lambda e: e.memset(small[:, C_EPS:C_EPS + 1], EPS), reads=[], writes=[buf("small")])
    P.op("pool", lambda e: e.memset(VA[:].rearrange("p a b c -> p (a b c)"), 1.0), writes=[buf("VA")])
    for k in range(8):
        P.op("dve", lambda e, k=k: e.tensor_scalar(out=gbc[:, k, :], in0=cmat[:, ONES, :],
                                                   scalar1=vecs[:, k:k + 1], scalar2=None, op0=ALU.mult),
             reads=[buf("cmat"), buf("vecs")], writes=[buf("gbc")])
    P.op("dve", lambda e: e.tensor_scalar(out=small[:, C_CSUB:C_CSUB + 1], in0=vecs[:, V_SUB:V_SUB + 1],
                                          scalar1=float(1.0 - LAM_INIT), scalar2=None, op0=ALU.mult),
         reads=[buf("vecs"), buf("small")], writes=[buf("small")])
    lam_tmp = xbuf[1]
    P.op("dve", lambda e: e.tensor_tensor(out=lam_tmp[:, 0:64], in0=lamt[:, 0, :], in1=lamt[:, 1, :], op=ALU.mult),
         reads=[buf("x2")], writes=[buf("x1")])
    P.op("dve", lambda e: e.tensor_tensor(out=lam_tmp[:, 64:128], in0=lamt[:, 2, :], in1=lamt[:, 3, :], op=ALU.mult),
         reads=[buf("x2")], writes=[buf("x1")])
    P.op("dve", lambda e: e.reduce_sum(out=small[:, C_S1:C_S1 + 1], in_=lam_tmp[:, 0:64], axis=mybir.AxisListType.X),
         reads=[buf("x1"), buf("small")], writes=[buf("small")])
    P.op("dve", lambda e: e.reduce_sum(out=small[:, C_S2:C_S2 + 1], in_=lam_tmp[:, 64:128], axis=mybir.AxisListType.X),
         reads=[buf("x1"), buf("small")], writes=[buf("small")])
    P.op("act", lambda e: e.activation(out=small[:, C_E1:C_E1 + 2], in_=small[:, C_S1:C_S1 + 2], func=AF.Exp),
         reads=[buf("small")], writes=[buf("small")])
    P.op("dve", lambda e: e.tensor_tensor(out=small[:, C_LAM:C_LAM + 1], in0=small[:, C_E1:C_E1 + 1],
                                          in1=small[:, C_E2:C_E2 + 1], op=ALU.subtract),
         reads=[buf("small")], writes=[buf("small")])
    P.op("dve", lambda e: e.tensor_scalar(out=small[:, C_LAM:C_LAM + 1], in0=small[:, C_LAM:C_LAM + 1],
                                          scalar1=float(LAM_INIT), scalar2=None, op0=ALU.add),
         reads=[buf("small")], writes=[buf("small")])
    P.op("dve", lambda e: e.tensor_scalar(out=small[:, C_NLAM:C_NLAM + 1], in0=small[:, C_LAM:C_LAM + 1],
                                          scalar1=-1.0, scalar2=None, op0=ALU.mult),
         reads=[buf("small")], writes=[buf("small")])
    for c in range(8):
        xb = xbuf[c % NXB]
        nm = "x%d" % (c % NXB)
        P.dma("sp", xb[:], wout_d[:, c, :], key=nm, writes=[buf(nm)])
        P.op("dve", lambda e, c=c, xb=xb: e.tensor_copy(out=wout[:, c, :], in_=xb[:]),
             reads=[buf(nm)], writes=[buf("wout")])
    state["x_rr"] = 8 % NXB

    prefetched = {}

    def prefetch_x(row_base):
        for tt in range(4):
            r = row_base + tt * 128
            if r not in prefetched:
                prefetched[r] = load_x(r, use_prefetched=False)

    def load_x(row0, use_prefetched=True):
        if use_prefetched and row0 in prefetched:
            return prefetched.pop(row0)
        i = state["x_rr"]
        state["x_rr"] = (i + 1) % NXB
        nm = "x%d" % i
        P.dma("sp", xbuf[i][:], xs[row0:row0 + 128, :], key=nm, writes=[buf(nm)])
        return xbuf[i], buf(nm)

    for ch in range(N_CH):
        si = ch % NWS
        wi = ch % NWB
        sn, wn = "x%d" % si, "wbf%d" % wi
        P.dma("sp", wst[si], wch[ch], key=sn, writes=[buf(sn)])
        P.op("dve" if ch % 3 else "pool", lambda e, si=si, wi=wi: e.tensor_tensor(
            out=wbf[wi][:].rearrange("p a b -> p (a b)"),
            in0=xbuf[si][:],
            in1=ropeb[:, 0:2, :].rearrange("p a b -> p (a b)"), op=ALU.mult),
            reads=[buf(sn), buf("gbc")], writes=[buf(wn)])
        P.dma("act", wsc[ch], wbf[wi][:], key="wsc_%s" % wn, reads=[buf(wn)], writes=[buf("wsc%d" % ch)])

    resident_w = {}

    def load_w(ch):
        if ch in resident_w:
            return resident_w[ch]
        wi = state["w_rr"]
        state["w_rr"] = (wi + 1) % NWB
        wn = "wbf%d" % wi
        P.dma("sp", wbf[wi][:], wsc[ch], key=wn, reads=[buf("wsc%d" % ch)], writes=[buf(wn)])
        return wbf[wi], buf(wn)

    def rstd_from(ms_ap, out_ap, scale, reads, writes, n_part=128):
        P.op("act", lambda e: e.activation(out=out_ap, in_=ms_ap, func=AF.Ln,
                                           bias=small[0:n_part, C_EPS:C_EPS + 1], scale=float(scale)),
             reads=list(reads) + [buf("small")], writes=writes)
        P.op("act", lambda e: e.activation(out=out_ap, in_=out_ap, func=AF.Exp, scale=-0.5),
             reads=writes, writes=writes)

    hbfs = [hbf, junk]
    C_SS4, C_RS4 = 11, 12

    def norm_transpose_block(row_base):
        xs_ = []
        banks_ = []

        def st_stats(tt):
            xb, xB = load_x(row_base + tt * 128)
            xs_.append((xb, xB))
            hb, hB = hbfs[tt % 2], buf("hbf%d" % (tt % 2))
            cs, cr = 11 + (tt % 2), 13 + (tt % 2)
            ssB, rsB = buf("ss%d" % (tt % 2)), buf("rstd%d" % (tt % 2))
            P.op("act", lambda e: e.activation(out=hb[:], in_=xb[:], func=AF.Square,
                                               accum_out=small[:, cs:cs + 1]),
                 reads=[xB], writes=[hB, ssB])
            rstd_from(small[:, cs:cs + 1], small[:, cr:cr + 1], 1.0 / D_MODEL, reads=[ssB], writes=[rsB])

        def st_scale(tt):
            xb, xB = xs_[tt]
            hb, hB = hbfs[tt % 2], buf("hbf%d" % (tt % 2))
            cr = 13 + (tt % 2)
            P.op("dve", lambda e: e.tensor_scalar(out=hb[:], in0=xb[:], scalar1=small[:, cr:cr + 1],
                                                  scalar2=None, op0=ALU.mult),
                 reads=[xB, buf("rstd%d" % (tt % 2))], writes=[hB])

        def st_tr(tt):
            hb, hB = hbfs[tt % 2], buf("hbf%d" % (tt % 2))
            b0 = next_bank(2)
            banks_.append(b0)
            for k in range(8):
                P.op("pe", lambda e, k=k: e.matmul(ps[:, b0 * 512 + k * 128: b0 * 512 + (k + 1) * 128],
                                                   lhsT=hb[:, k * 128:(k + 1) * 128], rhs=cmat[:, IDENT, :],
                                                   start=True, stop=True),
                     reads=[hB, buf("cmat")], writes=[bank[b0 + k // 4]])

        def st_evac(tt):
            b0 = banks_[tt]
            P.op("dve", lambda e: e.tensor_copy(
                out=hT[:, :, tt * 128:(tt + 1) * 128],
                in_=ps[:, b0 * 512:(b0 + 2) * 512].rearrange("p (k t) -> p k t", k=8)),
                reads=[bank[b0], bank[b0 + 1]], writes=[buf("hT")])

        st_stats(0)
        st_stats(1)
        st_scale(0)
        st_tr(0)
        st_scale(1)
        st_tr(1)
        st_stats(2)
        st_evac(0)
        st_stats(3)
        st_scale(2)
        st_tr(2)
        st_evac(1)
        st_scale(3)
        st_tr(3)
        st_evac(2)
        st_evac(3)

    def load_rope(t0):
        P.dma("sp", ropeb[:], rope_d[:, :, t0:t0 + QB].rearrange("f p t -> p f t"), key="rope",
              writes=[buf("rope"), buf("gbc")])

    def proj_feature(wt, wB):
        b = next_bank()
        for k in range(8):
            P.op("pe", lambda e, k=k, b=b: e.matmul(bk(b), lhsT=wt[:, k, :], rhs=hT[:, k, :],
                                                    start=(k == 0), stop=(k == 7)),
                 reads=(wB if isinstance(wB, list) else [wB]) + [buf("hT")], writes=[bank[b]])
        return b

    def rope_chunk(b, is_a, gcol, out_ap, outB):
        i = state["tmp_rr"]
        state["tmp_rr"] = (i + 1) % NTMP
        sq, qg, ta, tb, tc = t_sq[i], t_qg[i], t_a[i], t_b[i], t_c[0]
        sqB, qgB, taB, tbB, tcB = (buf("sq%d" % i), buf("qg%d" % i), buf("ta%d" % i),
                                   buf("tb%d" % i), buf("tc0"))
        ci, si = (0, 1) if is_a else (2, 3)
        rm = RMA if is_a else RMB
        bm = None
        if is_a:
            P.op("act", lambda e: e.activation(out=sq[:], in_=bk(b), func=AF.Square),
                 reads=[bank[b]], writes=[sqB])
            bm = next_bank()
            P.op("pe", lambda e: e.matmul(bk(bm), lhsT=cmat[:, BLK, :], rhs=sq[:], start=True, stop=True),
                 reads=[sqB, buf("cmat")], writes=[bank[bm]])
            P.op("act", lambda e: e.activation(out=qg[:], in_=bk(b), func=AF.Copy,
                                               scale=vecs[:, gcol:gcol + 1]),
                 reads=[bank[b], buf("vecs")], writes=[qgB])
            P.op("dve", lambda e: e.scalar_tensor_tensor(out=ta[:], in0=bk(b), scalar=vecs[:, gcol:gcol + 1],
                                                         in1=ropeb[:, ci, :], op0=ALU.mult, op1=ALU.mult),
                 reads=[bank[b], buf("vecs"), buf("rope")], writes=[taB])
        else:
            P.op("act", lambda e: e.activation(out=qg[:], in_=bk(b), func=AF.Copy), reads=[bank[b]], writes=[qgB])
            P.op("dve", lambda e: e.tensor_tensor(out=ta[:], in0=bk(b), in1=ropeb[:, ci, :], op=ALU.mult),
                 reads=[bank[b], buf("rope")], writes=[taB])
        br = next_bank()
        P.op("pe", lambda e: e.matmul(bk(br), lhsT=cmat[:, rm, :], rhs=qg[:], start=True, stop=True),
             reads=[qgB, buf("cmat")], writes=[bank[br]])

        def stage2():
            if is_a:
                rstd_from(bk(bm), tc[:], 1.0 / HEAD_DIM, reads=[bank[bm]], writes=[tcB])
            P.op("dve", lambda e: e.tensor_tensor(out=tb[:], in0=bk(br), in1=ropeb[:, si, :], op=ALU.mult),
                 reads=[bank[br], buf("rope")], writes=[tbB])
            if is_a:
                P.op("dve", lambda e: e.tensor_tensor(out=ta[:], in0=ta[:], in1=tb[:], op=ALU.add),
                     reads=[taB, tbB], writes=[taB])
                P.op("dve", lambda e: e.tensor_tensor(out=out_ap, in0=ta[:], in1=tc[:], op=ALU.mult),
                     reads=[taB, tcB], writes=[outB])
            else:
                P.op("dve", lambda e: e.tensor_tensor(out=out_ap, in0=ta[:], in1=tb[:], op=ALU.add),
                     reads=[taB, tbB], writes=[outB])
        return stage2

    def gate_chunk(b, c):
        i = state["tmp_rr"]
        state["tmp_rr"] = (i + 1) % NTMP
        ta = t_a[i]
        taB = buf("ta%d" % i)
        P.op("act", lambda e: e.activation(out=ta[:], in_=bk(b), func=AF.Tanh, scale=0.5),
             reads=[bank[b]], writes=[taB])
        P.op("dve", lambda e: e.scalar_tensor_tensor(out=gate[:, c, :], in0=ta[:], scalar=1.0, in1=bk(b),
                                                     op0=ALU.add, op1=ALU.mult),
             reads=[bank[b], taB], writes=[buf("gate")])

    def run_pipelined(tasks):
        prev = None
        pending2 = None
        for ch, proj, post in tasks:
            wt, wB = load_w(ch)
            b = proj(wt, wB)
            if prev is not None:
                p2 = prev[1](prev[0])
                if pending2 is not None:
                    pending2()
                pending2 = p2
            prev = (b, post)
        p2 = prev[1](prev[0])
        if pending2 is not None:
            pending2()
        if p2 is not None:
            p2()

    def do_sequence(base, S):
        nblk = S // QB
        nkb = S // 128
        if dbg == 0:
            return
        kslots = []
        for j in range(4):
            kslots.append((qT[:, :, j * 128:(j + 1) * 128], [buf("qT")]))
        for j in range(4):
            kslots.append((mixT[:, :, j * 128:(j + 1) * 128], [buf("mixT%d" % c_) for c_ in range(8)]))
        for j in range(2):
            kslots.append((gate[:, :, j * 128:(j + 1) * 128], [buf("gate")]))
        kchunks = [CH_KA] + [CH_KB + i for i in range(4)] + [CH_VA] + [CH_VB + i for i in range(4)]
        for i, ch in enumerate(kchunks):
            view, parents = kslots[i]
            kb_ = buf("kw%d" % i)
            prev_users = []
            for pb in parents:
                prev_users.extend(pb.readers())
                if pb.w is not None:
                    prev_users.append(pb.w)
            P.dma("sp", view, wsc[ch], key="kw%d" % i, reads=[buf("wsc%d" % ch)], writes=[kb_],
                  extra=prev_users)
            resident_w[ch] = (view, [kb_] + parents)
        for blk in range(nblk):
            t0 = blk * QB
            load_rope(t0)
            if dbg == 0.1:
                return
            norm_transpose_block(base + t0)
            if blk + 1 < nblk:
                prefetch_x(base + t0 + QB)
            else:
                prefetch_x(base)
            if dbg == 0.3:
                return
            tl0 = t0 // 128

            def v_proj(wt, wB):
                b = next_bank()
                for tt in range(4):
                    for k in range(8):
                        P.op("pe", lambda e, k=k, tt=tt: e.matmul(
                            ps[:, b * 512 + tt * 128: b * 512 + (tt + 1) * 128],
                            lhsT=hT[:, k, tt * 128:(tt + 1) * 128], rhs=wt[:, k, :],
                            start=(k == 0), stop=(k == 7)),
                            reads=(wB if isinstance(wB, list) else [wB]) + [buf("hT")], writes=[bank[b]])
                return b

            def va_evac(b):
                P.op("act", lambda e, tl0=tl0: e.activation(
                    out=VA[:, tl0:tl0 + 4, :, 0:64],
                    in_=bk(b).rearrange("p (t h d) -> p t h d", t=4, h=2), func=AF.Copy),
                    reads=[bank[b]], writes=[buf("VA")])

            def vb_evac(b, hb):
                P.op("act", lambda e, tl0=tl0: e.activation(
                    out=VB[:, tl0:tl0 + 4, hb * 128:(hb + 1) * 128],
                    in_=bk(b).rearrange("p (t c) -> p t c", t=4), func=AF.Copy),
                    reads=[bank[b]], writes=[buf("VB")])

            tasks = [(CH_KA, proj_feature, lambda b: rope_chunk(b, True, V_GK, kaT[:, t0:t0 + QB], buf("kaT")))]
            for i in range(4):
                tasks.append((CH_KB + i, proj_feature,
                              lambda b, i=i: rope_chunk(b, False, None, kbT[:, i, t0:t0 + QB], buf("kbT"))))
            tasks.append((CH_VA, v_proj, va_evac))
            for i in range(4):
                tasks.append((CH_VB + i, v_proj, lambda b, i=i: vb_evac(b, i)))
            run_pipelined(tasks)

        resident_w.clear()
        if dbg == 1:
            return
        def q_proj(blk):
            t0 = blk * QB
            load_rope(t0)
            norm_transpose_block(base + t0)
            tasks = []
            for j in range(4):
                tasks.append((CH_QA + j, proj_feature,
                              lambda b, j=j: rope_chunk(b, True, V_GQ, qT[:, j, :], buf("qT"))))
            for j in range(4):
                tasks.append((CH_QB + j, proj_feature,
                              lambda b, j=j: rope_chunk(b, False, None, qT[:, 4 + j, :], buf("qT"))))
            for j in range(8):
                tasks.append((CH_GA + j, proj_feature, lambda b, j=j: gate_chunk(b, j)))
            run_pipelined(tasks)

        q_proj(0)
        for blk in range(nblk):
            t0 = blk * QB
            if dbg == 2:
                return
            if blk + 1 < nblk:
                prefetch_x(base + t0 + QB)
            pairs = []
            for i in range(4):
                pairs.append((1, i))
                pairs.append((2, i))
            iters = [(pi, kb) for pi in range(len(pairs)) for kb in range(nkb)]
            st_tiles = [(0, 1), (2, 3)]
            deferred = {}
            BANK_O, BANK_S = 5, 6

            def a_bank(pi):
                return 4 if pi % 2 == 0 else 7

            def ksl(kb):
                return slice(kb * 128, (kb + 1) * 128)

            def halves(pi):
                typ, i = pairs[pi]
                return [("A", 0), ("B", 1)] if typ == 1 else [("B", 0), ("A", 1)]

            def emit_qk(n):
                pi, kb = iters[n]
                typ, i = pairs[pi]
                for kind, u in halves(pi):
                    bb = st_tiles[n % 2][u]
                    p0 = u * 64
                    if kind == "A":
                        P.op("pe", lambda e, bb=bb, p0=p0: e.matmul(
                            bk(bb), lhsT=kaT[p0:p0 + 64, ksl(kb)], rhs=qT[p0:p0 + 64, i, :],
                            start=True, stop=True),
                            reads=[buf("kaT"), buf("qT")], writes=[bank[bb]])
                    else:
                        P.op("pe", lambda e, bb=bb, p0=p0: e.matmul(
                            bk(bb), lhsT=kbT[p0:p0 + 64, i, ksl(kb)], rhs=qT[p0:p0 + 64, 4 + i, :],
                            start=True, stop=True),
                            reads=[buf("kbT"), buf("qT")], writes=[bank[bb]])

            def emit_exp(n):
                b0, b1 = st_tiles[n % 2]
                ei = state["e_rr"]
                state["e_rr"] = (ei + 1) % NE
                E, EB = Eb[ei], buf("E%d" % ei)
                P.op("act", lambda e: e.activation(out=E[:], in_=ps[:, b0 * 512:(b0 + 2) * 512],
                                                   func=AF.Exp, scale=0.125),
                     reads=[bank[b0], bank[b1]], writes=[EB])
                return E, EB

            def emit_pv(n, E, EB):
                pi, kb = iters[n]
                typ, i = pairs[pi]
                first = (kb == 0)
                last = (kb == nkb - 1)
                ab = a_bank(pi)
                for kind, u in halves(pi):
                    if kind == "A":
                        P.op("pe", lambda e, u=u: e.matmul(
                            bk(ab), lhsT=VA[:, kb, u, :], rhs=E[:, u * 512:(u + 1) * 512],
                            start=first, stop=last),
                            reads=[EB, buf("VA")], writes=[bank[ab]])
                    else:
                        P.op("pe", lambda e, u=u: e.matmul(
                            bk(BANK_O), lhsT=VB[:, kb, i * 128:(i + 1) * 128], rhs=E[:, u * 512:(u + 1) * 512],
                            start=first, stop=last),
                            reads=[EB, buf("VB")], writes=[bank[BANK_O]])
                        P.op("pe", lambda e, u=u: e.matmul(
                            bk(BANK_S), lhsT=cmat[:, ONES, :], rhs=E[:, u * 512:(u + 1) * 512],
                            start=first, stop=last),
                            reads=[EB, buf("cmat")], writes=[bank[BANK_S]])

            def finalize_pair(pi):
                typ, i = pairs[pi]
                ab = a_bank(pi)
                tO, tS = t_a[0], t_b[0]
                P.op("dve", lambda e: e.tensor_copy(out=tO[:], in_=bk(BANK_O)),
                     reads=[bank[BANK_O]], writes=[buf("ta0")])
                P.op("dve", lambda e: e.tensor_copy(out=tS[:], in_=bk(BANK_S)),
                     reads=[bank[BANK_S]], writes=[buf("tb0")])
                o_sb, s_sb = t_b[1], t_d
                P.op("dve", lambda e: e.tensor_copy(out=o_sb[0:64, :], in_=ps[0:64, ab * 512:(ab + 1) * 512]),
                     reads=[bank[ab]], writes=[buf("tb1")])
                P.op("dve", lambda e: e.tensor_copy(out=s_sb[0:64, :], in_=ps[64:128, ab * 512:(ab + 1) * 512]),
                     reads=[bank[ab]], writes=[buf("td")])
                P.op("dve", lambda e: e.reciprocal(out=tS[:], in_=tS[:]), reads=[buf("tb0")], writes=[buf("tb0")])
                if typ == 1:
                    P.op("dve", lambda e: e.tensor_tensor(out=t_a[1][:], in0=tO[:], in1=tS[:], op=ALU.mult),
                         reads=[buf("ta0"), buf("tb0")], writes=[buf("ta1")])
                else:
                    P.op("dve", lambda e: e.tensor_tensor(out=tO[:], in0=tO[:], in1=tS[:], op=ALU.mult),
                         reads=[buf("ta0"), buf("tb0")], writes=[buf("ta0")])
                    P.op("dve", lambda e: e.scalar_tensor_tensor(
                        out=t_ob[:], in0=t_a[1][:], scalar=small[:, C_NLAM:C_NLAM + 1], in1=tO[:],
                        op0=ALU.mult, op1=ALU.add),
                        reads=[buf("ta0"), buf("ta1"), buf("small")], writes=[buf("tc0")])
                    P.op("dve", lambda e: e.tensor_tensor(out=t_obsq[:], in0=t_ob[:], in1=t_ob[:], op=ALU.mult),
                         reads=[buf("tc0")], writes=[buf("sq0")])
                u = 0 if typ == 1 else 1
                p0 = u * 64
                P.op("dve", lambda e: e.reciprocal(out=s_sb[0:64, :], in_=s_sb[0:64, :]),
                     reads=[buf("td")], writes=[buf("td")])
                if p0 == 0:
                    P.op("dve", lambda e: e.tensor_tensor(out=o_sb[0:64, :], in0=o_sb[0:64, :], in1=s_sb[0:64, :],
                                                          op=ALU.mult),
                         reads=[buf("tb1"), buf("td")], writes=[buf("tb1")])
                    src, srcB = o_sb, buf("tb1")
                else:
                    P.op("dve", lambda e: e.tensor_tensor(out=s_sb[64:128, :], in0=o_sb[0:64, :], in1=s_sb[0:64, :],
                                                          op=ALU.mult),
                         reads=[buf("tb1"), buf("td")], writes=[buf("td")])
                    src, srcB = s_sb, buf("td")
                P.op("dve", lambda e: e.scalar_tensor_tensor(
                    out=mixT[p0:p0 + 64, i, :], in0=src[p0:p0 + 64, :], scalar=0.5, in1=gate[p0:p0 + 64, i, :],
                    op0=ALU.mult, op1=ALU.mult),
                    reads=[srcB, buf("gate")], writes=[buf("mixT%d" % i)])

            def finalize_b2a(i, bm):
                P.op("pe", lambda e: e.matmul(bk(bm), lhsT=cmat[:, ONES, :], rhs=t_obsq[:], start=True, stop=True),
                     reads=[buf("sq0"), buf("cmat")], writes=[bank[bm]])

            def finalize_b2b(i, bm):
                rstd_from(bk(bm), t_b[0][:], 1.0 / 128.0, reads=[bank[bm]], writes=[buf("tb0")])

            def finalize_b2c(i, bm):
                obB = buf("tc0")
                rs = t_b[0]
                P.op("dve", lambda e: e.scalar_tensor_tensor(
                    out=t_ob[:], in0=t_ob[:], scalar=small[:, C_CSUB:C_CSUB + 1], in1=rs[:],
                    op0=ALU.mult, op1=ALU.mult),
                    reads=[obB, buf("tb0"), buf("small")], writes=[obB])
                P.op("dve", lambda e: e.scalar_tensor_tensor(
                    out=mixT[:, 4 + i, :], in0=t_ob[:], scalar=0.5, in1=gate[:, 4 + i, :],
                    op0=ALU.mult, op1=ALU.mult),
                    reads=[obB, buf("gate")], writes=[buf("mixT%d" % (4 + i))])

            nit = len(iters)
            emit_qk(0)
            if nit > 1:
                emit_qk(1)
            for n in range(nit):
                E_, EB_ = emit_exp(n)
                if n + 2 < nit:
                    emit_qk(n + 2)
                emit_pv(n, E_, EB_)
                for fn_ in deferred.pop(n, []):
                    fn_()
                pi, kb = iters[n]
                if kb == nkb - 1:
                    typ, i = pairs[pi]
                    finalize_pair(pi)
                    if typ == 2:
                        bm = a_bank(pi)
                        d0 = max(1, min(10, nkb - 3))
                        for dd, fn2 in ((d0, finalize_b2a), (d0 + 2, finalize_b2b), (d0 + 3, finalize_b2c)):
                            deferred.setdefault(min(n + dd, nit - 1), []).append(
                                lambda i=i, bm=bm, fn2=fn2: fn2(i, bm))
            for k_ in sorted(deferred):
                for fn_ in deferred[k_]:
                    fn_()

            if dbg == 3:
                return
            if blk + 1 < nblk:
                q_proj(blk + 1)
            c_order = [0, 4, 1, 5, 2, 6, 3, 7]
            xts = []
            for tt in range(4):
                xts.append(load_x(base + t0 + tt * 128))
            for ci, c in enumerate(c_order):
                for tt in range(4):
                    for half in range(2):
                        bb = tt * 2 + half
                        P.op("pe", lambda e, c=c, half=half, bb=bb, tt=tt, ci=ci: e.matmul(
                            bk(bb), lhsT=mixT[:, c, tt * 128:(tt + 1) * 128],
                            rhs=wout[:, c, half * 512:(half + 1) * 512], start=(ci == 0), stop=(ci == 7)),
                            reads=[buf("mixT%d" % c), buf("wout")], writes=[bank[bb]])
            def ep_add(tt):
                xb, xB = xts[tt]
                b0 = tt * 2
                hb, hB = hbfs[tt % 2], buf("hbf%d" % (tt % 2))
                cs, cr = 16 + tt, 20 + tt
                ssB, rsB = buf("ess%d" % tt), buf("erstd%d" % tt)
                P.op("dve", lambda e: e.tensor_tensor(out=xb[:], in0=ps[:, b0 * 512:(b0 + 2) * 512],
                                                      in1=xb[:], op=ALU.add),
                     reads=[bank[b0], bank[b0 + 1], xB], writes=[xB])
                P.op("act", lambda e: e.activation(out=hb[:], in_=xb[:], func=AF.Square,
                                                   accum_out=small[:, cs:cs + 1]),
                     reads=[xB], writes=[hB, ssB])
                rstd_from(small[:, cs:cs + 1], small[:, cr:cr + 1], 1.0 / D_MODEL, reads=[ssB], writes=[rsB])

            def ep_scale(tt):
                xb, xB = xts[tt]
                cr = 20 + tt
                row0 = base + t0 + tt * 128
                P.op("dve", lambda e: e.scalar_tensor_tensor(
                    out=xb[:], in0=xb[:], scalar=small[:, cr:cr + 1], in1=gfin[:],
                    op0=ALU.mult, op1=ALU.mult),
                    reads=[xB, buf("erstd%d" % tt), buf("gfin")], writes=[xB])
                store_ops.append(P.dma("pool", ys[row0:row0 + 128, :], xb[:], key="st_" + xB.name, reads=[xB]))

            ep_add(0)
            ep_add(1)
            ep_add(2)
            ep_scale(0)
            ep_add(3)
            ep_scale(1)
            ep_scale(2)
            ep_scale(3)

    store_ops = []
    base = 0
    for S in SEQS:
        do_sequence(base, S)
        base += S
    P.op("sp", lambda e: None, extra=store_ops)
    return nc, P


def _rope_tables():
    f = np.float32
    t = np.arange(SMAX)
    row = (t // GRID_W).astype(f)
    col = (t % GRID_W).astype(f)
    pos = t.astype(f)
    inv16 = (f(ROPE_THETA) ** (-(np.arange(0, 32, 2).astype(f)) / f(32))).astype(f)
    inv32 = (f(ROPE_THETA) ** (-(np.arange(0, 64, 2).astype(f)) / f(64))).astype(f)
    tabs = np.zeros((4, 128, SMAX), f)
    for p in range(128):
        d = p % 64
        i = (d % 32) % 16
        ps_ = row if d < 32 else col
        ang = (ps_ * inv16[i]).astype(f)
        tabs[0, p] = np.cos(ang)
        tabs[1, p] = np.sin(ang)
        i2 = d % 32
        ang2 = (pos * inv32[i2]).astype(f)
        tabs[2, p] = np.cos(ang2)
        tabs[3, p] = np.sin(ang2)
    return tabs


def _const_mats():
    f = np.float32
    cm = np.zeros((128, 5, 128), f)
    cm[:, 0, :] = np.eye(128, dtype=f)
    cm[:, 1, :] = 1.0
    for p in range(128):
        for m in range(128):
            if p // 64 == m // 64:
                cm[p, 2, m] = 1.0
    for m in range(128):
        d = m % 32
        if d < 16:
            cm[m + 16, 3, m] = -1.0
        else:
            cm[m - 16, 3, m] = 1.0
        d2 = m % 64
        if d2 < 32:
            cm[m + 32, 4, m] = -1.0
        else:
            cm[m - 32, 4, m] = 1.0
    return cm


def _pack_weights(w_in, w_out):
    QA0, KA0, VA0, GA0, QB0, KB0, VB0, GB0 = 0, 512, 640, 768, 1280, 1792, 2304, 2816
    cols = []
    cols.append(np.arange(KA0, KA0 + 128))
    for i in range(4):
        cols.append(np.arange(KB0 + i * 128, KB0 + (i + 1) * 128))
    cols.append(np.arange(VA0, VA0 + 128))
    for i in range(4):
        cols.append(np.arange(VB0 + i * 128, VB0 + (i + 1) * 128))
    for j in range(4):
        cols.append(np.concatenate([np.arange(QA0 + j * 64, QA0 + (j + 1) * 64),
                                    np.arange(QA0 + (j + 4) * 64, QA0 + (j + 5) * 64)]))
    for i in range(4):
        cols.append(np.arange(QB0 + i * 128, QB0 + (i + 1) * 128))
    for j in range(4):
        cols.append(np.concatenate([np.arange(GA0 + j * 64, GA0 + (j + 1) * 64),
                                    np.arange(GA0 + (j + 4) * 64, GA0 + (j + 5) * 64)]))
    for i in range(4):
        cols.append(np.arange(GB0 + i * 128, GB0 + (i + 1) * 128))
    assert len(cols) == N_CH
    w = w_in[0]
    wch = np.empty((N_CH, 128, 8, 128), np.float32)
    wr = w.reshape(8, 128, -1)
    for c, cl in enumerate(cols):
        wch[c] = wr[:, :, cl].transpose(1, 0, 2)
    rows = []
    for j in range(4):
        rows.append(np.concatenate([np.arange(j * 64, (j + 1) * 64), np.arange((j + 4) * 64, (j + 5) * 64)]))
    for hb in range(4):
        rows.append(np.arange(512 + hb * 128, 512 + (hb + 1) * 128))
    wo = w_out[0]
    wout = np.empty((128, 8, 1024), np.float32)
    for c, rw in enumerate(rows):
        wout[:, c, :] = wo[rw, :]
    return wch, wout


_CACHE = {}


def kernel(x_prompt, x_sample, g_norm, w_in, a_q_norm, a_k_norm, b_lambda_q1, b_lambda_k1,
           b_lambda_q2, b_lambda_k2, b_subln, w_out, g_final):
    f = np.float32
    x_prompt = np.asarray(x_prompt, f)
    x_sample = np.asarray(x_sample, f)
    wch, wout = _pack_weights(np.asarray(w_in, f), np.asarray(w_out, f))
    vecs = np.zeros((128, 16), f)
    vecs[:, 0:8] = np.asarray(g_norm, f)[0].reshape(8, 128).T
    vecs[:, 8] = np.tile(np.asarray(a_q_norm, f)[0], 2)
    vecs[:, 9] = np.tile(np.asarray(a_k_norm, f)[0], 2)
    vecs[:, 10] = np.asarray(b_subln, f)[0]
    lamv = np.stack([np.asarray(v, f)[0] for v in (b_lambda_q1, b_lambda_k1, b_lambda_q2, b_lambda_k2)], 0)
    lamv = np.ascontiguousarray(np.broadcast_to(lamv[None], (128, 4, 64)))
    gfin = np.ascontiguousarray(np.broadcast_to(np.asarray(g_final, f)[None, :], (128, 1024)))
    if "tabs" not in _CACHE:
        _CACHE["tabs"] = (_rope_tables(), _const_mats())
    rope, cmat = _CACHE["tabs"]

    in_maps = []
    for c in range(N_CORES):
        xs = np.concatenate([x_prompt[2 * c], x_prompt[2 * c + 1], x_sample[c]], axis=0)
        in_maps.append({"xs": np.ascontiguousarray(xs), "wch": wch, "wout": wout, "vecs": vecs,
                        "lamv": lamv, "gfin": gfin, "rope": rope, "cmat": cmat})
    nc, P = build_program()
    P.emit()
    P.close()
    res = run_bass_kernel_spmd(nc, in_maps, core_ids=list(range(N_CORES)))
    y_prompt = np.empty_like(x_prompt)
    y_sample = np.empty_like(x_sample)
    for c in range(N_CORES):
        ys = res.results[c]["ys"]
        y_prompt[2 * c] = ys[0:2048]
        y_prompt[2 * c + 1] = ys[2048:4096]
        y_sample[c] = ys[4096:8192]
    return (y_prompt, y_sample)
```

```python
import contextlib
import numpy as np
import concourse.bass as bass
import concourse.mybir as mybir
from concourse.bass_utils import run_bass_kernel_spmd

F32 = mybir.dt.float32
BF16 = mybir.dt.bfloat16
ALU = mybir.AluOpType
AF = mybir.ActivationFunctionType

D_MODEL = 1024
HEAD_DIM = 64
GRID_W = 64
ROPE_THETA = 10000.0
EPS = 1e-6
N_CORES = 8
SEQS = (2048, 2048, 4096)
TOK = sum(SEQS)
SMAX = 4096
QB = 512
LAM_INIT = 0.8 - 0.6 * 1.0

ENGINES = ("pe", "act", "dve", "pool", "sp")


class Op:
    __slots__ = ("eng", "fn", "deps", "sig", "sem", "cnt", "idx", "dma_key", "raw")

    def __init__(self, eng, fn, deps, dma_key=None):
        self.raw = set(id(d) for d in deps)
        self.eng = eng
        self.fn = fn
        self.deps = deps
        self.sig = False
        self.sem = None
        self.cnt = 0
        self.idx = -1
        self.dma_key = dma_key


class Buf:
    __slots__ = ("name", "w", "r", "rd", "excl")

    def __init__(self, name, excl=False):
        self.name = name
        self.excl = excl
        self.w = None
        self.r = {}
        self.rd = []

    def readers(self):
        return list(self.r.values()) + self.rd


class Prog:
    def __init__(self, nc):
        self.nc = nc
        self.q = {e: [] for e in ENGINES}
        self.stack = contextlib.ExitStack()

    def sbuf(self, name, shape, dtype):
        return self.stack.enter_context(self.nc.sbuf_tensor(name, list(shape), dtype))

    def psum(self, name, shape, dtype):
        return self.stack.enter_context(self.nc.psum_tensor(name, list(shape), dtype))

    def _record(self, op, reads, writes):
        deps = op.deps
        raw = op.raw
        for b in reads:
            if b.w is not None:
                deps.append(b.w)
                raw.add(id(b.w))
            if b.excl:
                for en, r in b.r.items():
                    if en != op.eng:
                        deps.append(r)
        for b in writes:
            deps.extend(b.readers())
            if b.w is not None:
                deps.append(b.w)
        op.idx = len(self.q[op.eng])
        self.q[op.eng].append(op)
        for b in reads:
            if op.dma_key is not None:
                b.rd.append(op)
            else:
                b.r[op.eng] = op
        for b in writes:
            b.w = op
            b.r = {}
            b.rd = []
        return op

    def op(self, eng, fn, reads=(), writes=(), extra=()):
        return self._record(Op(eng, fn, [d for d in extra if d is not None]), reads, writes)

    def dma(self, eng, out, in_, key, reads=(), writes=(), extra=()):
        def fn(e, out=out, in_=in_):
            return e.dma_start(out=out, in_=in_)
        return self._record(Op(eng, fn, [d for d in extra if d is not None], dma_key=key),
                            reads, writes)

    def emit(self, self_gap=1 << 30):
        nc = self.nc
        st = self.stack
        for e in ENGINES:
            for op in self.q[e]:
                for d in op.deps:
                    if d.eng == e and d.dma_key is None and e == "pe":
                        continue
                    d.sig = True
                if op.dma_key is not None:
                    op.sig = True
        esem = {e: st.enter_context(nc.semaphore("sem_" + e)) for e in ENGINES}
        dsem, dcnt = {}, {}
        for e in ENGINES:
            c = 0
            for op in self.q[e]:
                if op.dma_key is not None:
                    k = op.dma_key
                    if k not in dsem:
                        dsem[k] = st.enter_context(nc.semaphore("dsem_%s" % (k,)))
                        dcnt[k] = 0
                    dcnt[k] += 16
                    op.sem, op.cnt = dsem[k], dcnt[k]
                elif op.sig:
                    c += 1
                    op.sem, op.cnt = esem[e], c
        block = st.enter_context(nc.Block())
        prog = self

        def run(e, eng_obj):
            waited = {}
            for op in prog.q[e]:
                need = {}
                for d in op.deps:
                    if d.eng == e and d.dma_key is None:
                        if e == "pe" or op.idx - d.idx >= self_gap:
                            continue
                    key = id(d.sem)
                    if key not in need or need[key][1] < d.cnt:
                        need[key] = (d.sem, d.cnt)
                for key, (sem, cnt) in need.items():
                    if waited.get(key, 0) >= cnt:
                        continue
                    eng_obj.wait_ge(sem, cnt)
                    waited[key] = cnt
                ins = op.fn(eng_obj)
                if op.sig:
                    ins.then_inc(op.sem, 16 if op.dma_key is not None else 1)

        @block.tensor
        def _(eng):
            run("pe", eng)

        @block.scalar
        def _(eng):
            run("act", eng)

        @block.vector
        def _(eng):
            run("dve", eng)

        @block.gpsimd
        def _(eng):
            run("pool", eng)

        @block.sync
        def _(eng):
            run("sp", eng)

    def close(self):
        self.stack.close()


CH_KA = 0
CH_KB = 1
CH_VA = 5
CH_VB = 6
CH_QA = 10
CH_QB = 14
CH_GA = 18
CH_GB = 22
N_CH = 26


def build_program(SEQS=SEQS, dbg=None):
    TOK = sum(SEQS)
    nc = bass.Bass("TRN2", target_bir_lowering=False)
    xs = nc.dram_tensor("xs", [TOK, D_MODEL], F32, kind="ExternalInput").ap()
    wch = nc.dram_tensor("wch", [N_CH, 128, 8, 128], F32, kind="ExternalInput").ap()
    wout_d = nc.dram_tensor("wout", [128, 8, 1024], F32, kind="ExternalInput").ap()
    vecs_d = nc.dram_tensor("vecs", [128, 16], F32, kind="ExternalInput").ap()
    lamv_d = nc.dram_tensor("lamv", [128, 4, 64], F32, kind="ExternalInput").ap()
    gfin_d = nc.dram_tensor("gfin", [128, 1024], F32, kind="ExternalInput").ap()
    rope_d = nc.dram_tensor("rope", [4, 128, SMAX], F32, kind="ExternalInput").ap()
    cmat_d = nc.dram_tensor("cmat", [128, 5, 128], F32, kind="ExternalInput").ap()
    ys = nc.dram_tensor("ys", [TOK, D_MODEL], F32, kind="ExternalOutput").ap()
    wsc = nc.dram_tensor("wsc", [N_CH, 128, 8, 128], BF16, kind="Internal").ap()

    P = Prog(nc)
    NT = SMAX // 128

    wout = P.sbuf("wout_bf", [128, 8, 1024], BF16)
    cmat = P.sbuf("cmat_bf", [128, 5, 128], BF16)
    vecs = P.sbuf("vecs_sb", [128, 16], F32)
    gfin = P.sbuf("gfin_sb", [128, 1024], F32)
    small = P.sbuf("small", [128, 32], F32)
    kaT = P.sbuf("kaT", [128, SMAX], BF16)
    kbT = P.sbuf("kbT", [128, 4, SMAX], BF16)
    VA = P.sbuf("VA", [128, NT, 2, 128], BF16)
    VB = P.sbuf("VB", [128, NT, 512], BF16)
    NXB = 4
    xbuf = [P.sbuf("xbuf%d" % i, [128, 1024], F32) for i in range(NXB)]
    junk = P.sbuf("junk", [128, 1024], BF16)
    hbf = P.sbuf("hbf", [128, 1024], BF16)
    hT = P.sbuf("hT", [128, 8, QB], BF16)
    ropeb = P.sbuf("ropeb", [128, 4, QB], F32)
    gbc = ropeb[:, 0:2, :].rearrange("p a (b c) -> p (a b) c", c=128)
    qT = P.sbuf("qT", [128, 8, QB], BF16)
    gate = P.sbuf("gate", [128, 8, QB], BF16)
    mixT = P.sbuf("mixT", [128, 8, QB], BF16)
    NE = 4
    Eb = [P.sbuf("E%d" % i, [128, 1024], BF16) for i in range(NE)]
    NWS = NXB
    wst = [xb_[:].rearrange("p (a b) -> p a b", a=8) for xb_ in xbuf]
    NWB = 4
    wbf = [P.sbuf("wbf%d" % i, [128, 8, 128], BF16) for i in range(NWB)]
    NTMP = 2
    t_sq = [P.sbuf("t_sq%d" % i, [128, QB], BF16) for i in range(NTMP)]
    t_qg = [P.sbuf("t_qg%d" % i, [128, QB], BF16) for i in range(NTMP)]
    t_a = [P.sbuf("t_a%d" % i, [128, QB], F32) for i in range(NTMP)]
    t_b = [P.sbuf("t_b%d" % i, [128, QB], F32) for i in range(NTMP)]
    t_c = [P.sbuf("t_c0", [128, QB], F32)]
    t_d = P.sbuf("t_d", [128, QB], F32)
    t_on, t_r, t_ob, t_obsq = t_a, t_b, t_c[0], t_sq[0]
    ps = P.psum("ps", [128, 4096], F32)

    IDENT, ONES, BLK, RMA, RMB = 0, 1, 2, 3, 4
    C_EPS, C_LAM, C_NLAM, C_CSUB, C_SS, C_LN, C_RSTD, C_E1, C_E2, C_S1, C_S2 = range(11)
    V_GQ, V_GK, V_SUB = 8, 9, 10

    B = {}

    def buf(name):
        if name not in B:
            B[name] = Buf(name)
        return B[name]

    bank = [buf("bank%d" % i) for i in range(8)]
    for bb_ in bank:
        bb_.excl = True

    def bk(i):
        return ps[:, i * 512:(i + 1) * 512]

    state = {"bank_rr": 0, "x_rr": 0, "w_rr": 0, "ws_rr": 0, "tmp_rr": 0, "e_rr": 0}

    def next_bank(n=1):
        i = state["bank_rr"]
        if n == 2 and i % 2 == 1:
            i = (i + 1) % 8
        state["bank_rr"] = (i + n) % 8
        return i

    xb0 = xbuf[0]
    lamt = xbuf[2][:, 0:256].rearrange("p (a b) -> p a b", a=4)
    P.dma("sp", xb0[:, 0:640], cmat_d.rearrange("p a b -> p (a b)"), key="x0", writes=[buf("x0")])
    P.op("dve", lambda e: e.tensor_copy(out=cmat[:].rearrange("p a b -> p (a b)"), in_=xb0[:, 0:640]),
         reads=[buf("x0")], writes=[buf("cmat")])
    P.dma("sp", vecs[:], vecs_d, key="vecs", writes=[buf("vecs")])
    P.dma("sp", lamt, lamv_d, key="x2", writes=[buf("x2")])
    P.dma("sp", gfin[:], gfin_d, key="gfin", writes=[buf("gfin")])
    P.op("pool", lambda e: e.memset(small[:], 0.0), writes=[buf("small"), buf("ss"), buf("rstd"), buf("ss0"), buf("ss1"), buf("rstd0"), buf("rstd1")] + [buf("ess%d" % t_) for t_ in range(4)] + [buf("erstd%d" % t_) for t_ in range(4)])
    P.op("pool", lambda e: e.memset(small[:, C_EPS:C_EPS + 1], EPS), reads=[], writes=[buf("small")])
    P.op("pool", lambda e: e.memset(VA[:].rearrange("p a b c -> p (a b c)"), 1.0), writes=[buf("VA")])
    for k in range(8):
        P.op("dve", lambda e, k=k: e.tensor_scalar(out=gbc[:, k, :], in0=cmat[:, ONES, :],
                                                   scalar1=vecs[:, k:k + 1], scalar2=None, op0=ALU.mult),
             reads=[buf("cmat"), buf("vecs")], writes=[buf("gbc")])
    P.op("dve", lambda e: e.tensor_scalar(out=small[:, C_CSUB:C_CSUB + 1], in0=vecs[:, V_SUB:V_SUB + 1],
                                          scalar1=float(1.0 - LAM_INIT), scalar2=None, op0=ALU.mult),
         reads=[buf("vecs"), buf("small")], writes=[buf("small")])
    lam_tmp = xbuf[1]
    P.op("dve", lambda e: e.tensor_tensor(out=lam_tmp[:, 0:64], in0=lamt[:, 0, :], in1=lamt[:, 1, :], op=ALU.mult),
         reads=[buf("x2")], writes=[buf("x1")])
    P.op("dve", lambda e: e.tensor_tensor(out=lam_tmp[:, 64:128], in0=lamt[:, 2, :], in1=lamt[:, 3, :], op=ALU.mult),
         reads=[buf("x2")], writes=[buf("x1")])
    P.op("dve", lambda e: e.reduce_sum(out=small[:, C_S1:C_S1 + 1], in_=lam_tmp[:, 0:64], axis=mybir.AxisListType.X),
         reads=[buf("x1"), buf("small")], writes=[buf("small")])
    P.op("dve", lambda e: e.reduce_sum(out=small[:, C_S2:C_S2 + 1], in_=lam_tmp[:, 64:128], axis=mybir.AxisListType.X),
         reads=[buf("x1"), buf("small")], writes=[buf("small")])
    P.op("act", lambda e: e.activation(out=small[:, C_E1:C_E1 + 2], in_=small[:, C_S1:C_S1 + 2], func=AF.Exp),
         reads=[buf("small")], writes=[buf("small")])
    P.op("dve", lambda e: e.tensor_tensor(out=small[:, C_LAM:C_LAM + 1], in0=small[:, C_E1:C_E1 + 1],
                                          in1=small[:, C_E2:C_E2 + 1], op=ALU.subtract),
         reads=[buf("small")], writes=[buf("small")])
    P.op("dve", lambda e: e.tensor_scalar(out=small[:, C_LAM:C_LAM + 1], in0=small[:, C_LAM:C_LAM + 1],
                                          scalar1=float(LAM_INIT), scalar2=None, op0=ALU.add),
         reads=[buf("small")], writes=[buf("small")])
    P.op("dve", lambda e: e.tensor_scalar(out=small[:, C_NLAM:C_NLAM + 1], in0=small[:, C_LAM:C_LAM + 1],
                                          scalar1=-1.0, scalar2=None, op0=ALU.mult),
         reads=[buf("small")], writes=[buf("small")])
    for c in range(8):
        xb = xbuf[c % NXB]
        nm = "x%d" % (c % NXB)
        P.dma("sp", xb[:], wout_d[:, c, :], key=nm, writes=[buf(nm)])
        P.op("dve", lambda e, c=c, xb=xb: e.tensor_copy(out=wout[:, c, :], in_=xb[:]),
             reads=[buf(nm)], writes=[buf("wout")])
    state["x_rr"] = 8 % NXB

    prefetched = {}

    def prefetch_x(row_base):
        for tt in range(4):
            r = row_base + tt * 128
            if r not in prefetched:
                prefetched[r] = load_x(r, use_prefetched=False)

    def load_x(row0, use_prefetched=True):
        if use_prefetched and row0 in prefetched:
            return prefetched.pop(row0)
        i = state["x_rr"]
        state["x_rr"] = (i + 1) % NXB
        nm = "x%d" % i
        P.dma("sp", xbuf[i][:], xs[row0:row0 + 128, :], key=nm, writes=[buf(nm)])
        return xbuf[i], buf(nm)

    for ch in range(N_CH):
        si = ch % NWS
        wi = ch % NWB
        sn, wn = "x%d" % si, "wbf%d" % wi
        P.dma("sp", wst[si], wch[ch], key=sn, writes=[buf(sn)])
        P.op("dve" if ch % 3 else "pool", lambda e, si=si, wi=wi: e.tensor_tensor(
            out=wbf[wi][:].rearrange("p a b -> p (a b)"),
            in0=xbuf[si][:],
            in1=ropeb[:, 0:2, :].rearrange("p a b -> p (a b)"), op=ALU.mult),
            reads=[buf(sn), buf("gbc")], writes=[buf(wn)])
        P.dma("act", wsc[ch], wbf[wi][:], key="wsc_%s" % wn, reads=[buf(wn)], writes=[buf("wsc%d" % ch)])

    resident_w = {}

    def load_w(ch):
        if ch in resident_w:
            return resident_w[ch]
        wi = state["w_rr"]
        state["w_rr"] = (wi + 1) % NWB
        wn = "wbf%d" % wi
        P.dma("sp", wbf[wi][:], wsc[ch], key=wn, reads=[buf("wsc%d" % ch)], writes=[buf(wn)])
        return wbf[wi], buf(wn)

    def rstd_from(ms_ap, out_ap, scale, reads, writes, n_part=128):
        P.op("act", lambda e: e.activation(out=out_ap, in_=ms_ap, func=AF.Ln,
                                           bias=small[0:n_part, C_EPS:C_EPS + 1], scale=float(scale)),
             reads=list(reads) + [buf("small")], writes=writes)
        P.op("act", lambda e: e.activation(out=out_ap, in_=out_ap, func=AF.Exp, scale=-0.5),
             reads=writes, writes=writes)

    hbfs = [hbf, junk]
    C_SS4, C_RS4 = 11, 12

    def norm_transpose_block(row_base):
        xs_ = []
        banks_ = []

        def st_stats(tt):
            xb, xB = load_x(row_base + tt * 128)
            xs_.append((xb, xB))
            hb, hB = hbfs[tt % 2], buf("hbf%d" % (tt % 2))
            cs, cr = 11 + (tt % 2), 13 + (tt % 2)
            ssB, rsB = buf("ss%d" % (tt % 2)), buf("rstd%d" % (tt % 2))
            P.op("act", lambda e: e.activation(out=hb[:], in_=xb[:], func=AF.Square,
                                               accum_out=small[:, cs:cs + 1]),
                 reads=[xB], writes=[hB, ssB])
            rstd_from(small[:, cs:cs + 1], small[:, cr:cr + 1], 1.0 / D_MODEL, reads=[ssB], writes=[rsB])

        def st_scale(tt):
            xb, xB = xs_[tt]
            hb, hB = hbfs[tt % 2], buf("hbf%d" % (tt % 2))
            cr = 13 + (tt % 2)
            P.op("dve", lambda e: e.tensor_scalar(out=hb[:], in0=xb[:], scalar1=small[:, cr:cr + 1],
                                                  scalar2=None, op0=ALU.mult),
                 reads=[xB, buf("rstd%d" % (tt % 2))], writes=[hB])

        def st_tr(tt):
            hb, hB = hbfs[tt % 2], buf("hbf%d" % (tt % 2))
            b0 = next_bank(2)
            banks_.append(b0)
            for k in range(8):
                P.op("pe", lambda e, k=k: e.matmul(ps[:, b0 * 512 + k * 128: b0 * 512 + (k + 1) * 128],
                                                   lhsT=hb[:, k * 128:(k + 1) * 128], rhs=cmat[:, IDENT, :],
                                                   start=True, stop=True),
                     reads=[hB, buf("cmat")], writes=[bank[b0 + k // 4]])

        def st_evac(tt):
            b0 = banks_[tt]
            P.op("dve", lambda e: e.tensor_copy(
                out=hT[:, :, tt * 128:(tt + 1) * 128],
                in_=ps[:, b0 * 512:(b0 + 2) * 512].rearrange("p (k t) -> p k t", k=8)),
                reads=[bank[b0], bank[b0 + 1]], writes=[buf("hT")])

        st_stats(0)
        st_stats(1)
        st_scale(0)
        st_tr(0)
        st_scale(1)
        st_tr(1)
        st_stats(2)
        st_evac(0)
        st_stats(3)
        st_scale(2)
        st_tr(2)
        st_evac(1)
        st_scale(3)
        st_tr(3)
        st_evac(2)
        st_evac(3)

    def load_rope(t0):
        P.dma("sp", ropeb[:], rope_d[:, :, t0:t0 + QB].rearrange("f p t -> p f t"), key="rope",
              writes=[buf("rope"), buf("gbc")])

    def proj_feature(wt, wB):
        b = next_bank()
        for k in range(8):
            P.op("pe", lambda e, k=k, b=b: e.matmul(bk(b), lhsT=wt[:, k, :], rhs=hT[:, k, :],
                                                    start=(k == 0), stop=(k == 7)),
                 reads=(wB if isinstance(wB, list) else [wB]) + [buf("hT")], writes=[bank[b]])
        return b

    def rope_chunk(b, is_a, gcol, out_ap, outB):
        i = state["tmp_rr"]
        state["tmp_rr"] = (i + 1) % NTMP
        sq, qg, ta, tb, tc = t_sq[i], t_qg[i], t_a[i], t_b[i], t_c[0]
        sqB, qgB, taB, tbB, tcB = (buf("sq%d" % i), buf("qg%d" % i), buf("ta%d" % i),
                                   buf("tb%d" % i), buf("tc0"))
        ci, si = (0, 1) if is_a else (2, 3)
        rm = RMA if is_a else RMB
        bm = None
        if is_a:
            P.op("act", lambda e: e.activation(out=sq[:], in_=bk(b), func=AF.Square),
                 reads=[bank[b]], writes=[sqB])
            bm = next_bank()
            P.op("pe", lambda e: e.matmul(bk(bm), lhsT=cmat[:, BLK, :], rhs=sq[:], start=True, stop=True),
                 reads=[sqB, buf("cmat")], writes=[bank[bm]])
            P.op("act", lambda e: e.activation(out=qg[:], in_=bk(b), func=AF.Copy,
                                               scale=vecs[:, gcol:gcol + 1]),
                 reads=[bank[b], buf("vecs")], writes=[qgB])
            P.op("dve", lambda e: e.scalar_tensor_tensor(out=ta[:], in0=bk(b), scalar=vecs[:, gcol:gcol + 1],
                                                         in1=ropeb[:, ci, :], op0=ALU.mult, op1=ALU.mult),
                 reads=[bank[b], buf("vecs"), buf("rope")], writes=[taB])
        else:
            P.op("act", lambda e: e.activation(out=qg[:], in_=bk(b), func=AF.Copy), reads=[bank[b]], writes=[qgB])
            P.op("dve", lambda e: e.tensor_tensor(out=ta[:], in0=bk(b), in1=ropeb[:, ci, :], op=ALU.mult),
                 reads=[bank[b], buf("rope")], writes=[taB])
        br = next_bank()
        P.op("pe", lambda e: e.matmul(bk(br), lhsT=cmat[:, rm, :], rhs=qg[:], start=True, stop=True),
             reads=[qgB, buf("cmat")], writes=[bank[br]])

        def stage2():
            if is_a:
                rstd_from(bk(bm), tc[:], 1.0 / HEAD_DIM, reads=[bank[bm]], writes=[tcB])
            P.op("dve", lambda e: e.tensor_tensor(out=tb[:], in0=bk(br), in1=ropeb[:, si, :], op=ALU.mult),
                 reads=[bank[br], buf("rope")], writes=[tbB])
            if is_a:
                P.op("dve", lambda e: e.tensor_tensor(out=ta[:], in0=ta[:], in1=tb[:], op=ALU.add),
                     reads=[taB, tbB], writes=[taB])
                P.op("dve", lambda e: e.tensor_tensor(out=out_ap, in0=ta[:], in1=tc[:], op=ALU.mult),
                     reads=[taB, tcB], writes=[outB])
            else:
                P.op("dve", lambda e: e.tensor_tensor(out=out_ap, in0=ta[:], in1=tb[:], op=ALU.add),
                     reads=[taB, tbB], writes=[outB])
        return stage2

    def gate_chunk(b, c):
        i = state["tmp_rr"]
        state["tmp_rr"] = (i + 1) % NTMP
        ta = t_a[i]
        taB = buf("ta%d" % i)
        P.op("act", lambda e: e.activation(out=ta[:], in_=bk(b), func=AF.Tanh, scale=0.5),
             reads=[bank[b]], writes=[taB])
        P.op("dve", lambda e: e.scalar_tensor_tensor(out=gate[:, c, :], in0=ta[:], scalar=1.0, in1=bk(b),
                                                     op0=ALU.add, op1=ALU.mult),
             reads=[bank[b], taB], writes=[buf("gate")])

    def run_pipelined(tasks):
        prev = None
        pending2 = None
        for ch, proj, post in tasks:
            wt, wB = load_w(ch)
            b = proj(wt, wB)
            if prev is not None:
                p2 = prev[1](prev[0])
                if pending2 is not None:
                    pending2()
                pending2 = p2
            prev = (b, post)
        p2 = prev[1](prev[0])
        if pending2 is not None:
            pending2()
        if p2 is not None:
            p2()

    def do_sequence(base, S):
        nblk = S // QB
        nkb = S // 128
        if dbg == 0:
            return
        for blk in range(nblk):
            t0 = blk * QB
            load_rope(t0)
            if dbg == 0.1:
                return
            norm_transpose_block(base + t0)
            if blk == 0:
                kslots = []
                for j in range(4):
                    kslots.append((qT[:, :, j * 128:(j + 1) * 128], [buf("qT")]))
                for j in range(4):
                    kslots.append((mixT[:, :, j * 128:(j + 1) * 128], [buf("mixT%d" % c_) for c_ in range(8)]))
                for j in range(2):
                    kslots.append((gate[:, :, j * 128:(j + 1) * 128], [buf("gate")]))
                kchunks = [CH_KA] + [CH_KB + i for i in range(4)] + [CH_VA] + [CH_VB + i for i in range(4)]
                for i, ch in enumerate(kchunks):
                    view, parents = kslots[i]
                    kb_ = buf("kw%d" % i)
                    prev_users = []
                    for pb in parents:
                        prev_users.extend(pb.readers())
                        if pb.w is not None:
                            prev_users.append(pb.w)
                    P.dma("sp", view, wsc[ch], key="kw%d" % i, reads=[buf("wsc%d" % ch)], writes=[kb_],
                          extra=prev_users)
                    resident_w[ch] = (view, [kb_] + parents)
            if blk + 1 < nblk:
                prefetch_x(base + t0 + QB)
            else:
                prefetch_x(base)
            if dbg == 0.3:
                return
            tl0 = t0 // 128

            def v_proj(wt, wB):
                b = next_bank()
                for tt in range(4):
                    for k in range(8):
                        P.op("pe", lambda e, k=k, tt=tt: e.matmul(
                            ps[:, b * 512 + tt * 128: b * 512 + (tt + 1) * 128],
                            lhsT=hT[:, k, tt * 128:(tt + 1) * 128], rhs=wt[:, k, :],
                            start=(k == 0), stop=(k == 7)),
                            reads=(wB if isinstance(wB, list) else [wB]) + [buf("hT")], writes=[bank[b]])
                return b

            def va_evac(b):
                P.op("act", lambda e, tl0=tl0: e.activation(
                    out=VA[:, tl0:tl0 + 4, :, 0:64],
                    in_=bk(b).rearrange("p (t h d) -> p t h d", t=4, h=2), func=AF.Copy),
                    reads=[bank[b]], writes=[buf("VA")])

            def vb_evac(b, hb):
                P.op("act", lambda e, tl0=tl0: e.activation(
                    out=VB[:, tl0:tl0 + 4, hb * 128:(hb + 1) * 128],
                    in_=bk(b).rearrange("p (t c) -> p t c", t=4), func=AF.Copy),
                    reads=[bank[b]], writes=[buf("VB")])

            tasks = [(CH_KA, proj_feature, lambda b: rope_chunk(b, True, V_GK, kaT[:, t0:t0 + QB], buf("kaT")))]
            for i in range(4):
                tasks.append((CH_KB + i, proj_feature,
                              lambda b, i=i: rope_chunk(b, False, None, kbT[:, i, t0:t0 + QB], buf("kbT"))))
            tasks.append((CH_VA, v_proj, va_evac))
            for i in range(4):
                tasks.append((CH_VB + i, v_proj, lambda b, i=i: vb_evac(b, i)))
            run_pipelined(tasks)

        resident_w.clear()
        if dbg == 1:
            return
        def q_proj(blk):
            t0 = blk * QB
            load_rope(t0)
            norm_transpose_block(base + t0)
            tasks = []
            for j in range(4):
                tasks.append((CH_QA + j, proj_feature,
                              lambda b, j=j: rope_chunk(b, True, V_GQ, qT[:, j, :], buf("qT"))))
            for j in range(4):
                tasks.append((CH_QB + j, proj_feature,
                              lambda b, j=j: rope_chunk(b, False, None, qT[:, 4 + j, :], buf("qT"))))
            for j in range(8):
                tasks.append((CH_GA + j, proj_feature, lambda b, j=j: gate_chunk(b, j)))
            run_pipelined(tasks)

        q_proj(0)
        for blk in range(nblk):
            t0 = blk * QB
            if dbg == 2:
                return
            if blk + 1 < nblk:
                prefetch_x(base + t0 + QB)
            pairs = []
            for i in range(4):
                pairs.append((1, i))
                pairs.append((2, i))
            iters = [(pi, kb) for pi in range(len(pairs)) for kb in range(nkb)]
            st_tiles = [(0, 1), (2, 3)]
            deferred = {}
            BANK_O, BANK_S = 5, 6

            def a_bank(pi):
                return 4 if pi % 2 == 0 else 7

            def ksl(kb):
                return slice(kb * 128, (kb + 1) * 128)

            def halves(pi):
                typ, i = pairs[pi]
                return [("A", 0), ("B", 1)] if typ == 1 else [("B", 0), ("A", 1)]

            def emit_qk(n):
                pi, kb = iters[n]
                typ, i = pairs[pi]
                for kind, u in halves(pi):
                    bb = st_tiles[n % 2][u]
                    p0 = u * 64
                    if kind == "A":
                        P.op("pe", lambda e, bb=bb, p0=p0: e.matmul(
                            bk(bb), lhsT=kaT[p0:p0 + 64, ksl(kb)], rhs=qT[p0:p0 + 64, i, :],
                            start=True, stop=True),
                            reads=[buf("kaT"), buf("qT")], writes=[bank[bb]])
                    else:
                        P.op("pe", lambda e, bb=bb, p0=p0: e.matmul(
                            bk(bb), lhsT=kbT[p0:p0 + 64, i, ksl(kb)], rhs=qT[p0:p0 + 64, 4 + i, :],
                            start=True, stop=True),
                            reads=[buf("kbT"), buf("qT")], writes=[bank[bb]])

            def emit_exp(n):
                b0, b1 = st_tiles[n % 2]
                ei = state["e_rr"]
                state["e_rr"] = (ei + 1) % NE
                E, EB = Eb[ei], buf("E%d" % ei)
                P.op("act", lambda e: e.activation(out=E[:], in_=ps[:, b0 * 512:(b0 + 2) * 512],
                                                   func=AF.Exp, scale=0.125),
                     reads=[bank[b0], bank[b1]], writes=[EB])
                return E, EB

            def emit_pv(n, E, EB):
                pi, kb = iters[n]
                typ, i = pairs[pi]
                first = (kb == 0)
                last = (kb == nkb - 1)
                ab = a_bank(pi)
                for kind, u in halves(pi):
                    if kind == "A":
                        P.op("pe", lambda e, u=u: e.matmul(
                            bk(ab), lhsT=VA[:, kb, u, :], rhs=E[:, u * 512:(u + 1) * 512],
                            start=first, stop=last),
                            reads=[EB, buf("VA")], writes=[bank[ab]])
                    else:
                        P.op("pe", lambda e, u=u: e.matmul(
                            bk(BANK_O), lhsT=VB[:, kb, i * 128:(i + 1) * 128], rhs=E[:, u * 512:(u + 1) * 512],
                            start=first, stop=last),
                            reads=[EB, buf("VB")], writes=[bank[BANK_O]])
                        P.op("pe", lambda e, u=u: e.matmul(
                            bk(BANK_S), lhsT=cmat[:, ONES, :], rhs=E[:, u * 512:(u + 1) * 512],
                            start=first, stop=last),
                            reads=[EB, buf("cmat")], writes=[bank[BANK_S]])

            def finalize_pair(pi):
                typ, i = pairs[pi]
                ab = a_bank(pi)
                tO, tS = t_a[0], t_b[0]
                P.op("dve", lambda e: e.tensor_copy(out=tO[:], in_=bk(BANK_O)),
                     reads=[bank[BANK_O]], writes=[buf("ta0")])
                P.op("dve", lambda e: e.tensor_copy(out=tS[:], in_=bk(BANK_S)),
                     reads=[bank[BANK_S]], writes=[buf("tb0")])
                o_sb, s_sb = t_b[1], t_d
                P.op("dve", lambda e: e.tensor_copy(out=o_sb[0:64, :], in_=ps[0:64, ab * 512:(ab + 1) * 512]),
                     reads=[bank[ab]], writes=[buf("tb1")])
                P.op("dve", lambda e: e.tensor_copy(out=s_sb[0:64, :], in_=ps[64:128, ab * 512:(ab + 1) * 512]),
                     reads=[bank[ab]], writes=[buf("td")])
                P.op("dve", lambda e: e.reciprocal(out=tS[:], in_=tS[:]), reads=[buf("tb0")], writes=[buf("tb0")])
                if typ == 1:
                    P.op("dve", lambda e: e.tensor_tensor(out=t_a[1][:], in0=tO[:], in1=tS[:], op=ALU.mult),
                         reads=[buf("ta0"), buf("tb0")], writes=[buf("ta1")])
                else:
                    P.op("dve", lambda e: e.tensor_tensor(out=tO[:], in0=tO[:], in1=tS[:], op=ALU.mult),
                         reads=[buf("ta0"), buf("tb0")], writes=[buf("ta0")])
                    P.op("dve", lambda e: e.scalar_tensor_tensor(
                        out=t_ob[:], in0=t_a[1][:], scalar=small[:, C_NLAM:C_NLAM + 1], in1=tO[:],
                        op0=ALU.mult, op1=ALU.add),
                        reads=[buf("ta0"), buf("ta1"), buf("small")], writes=[buf("tc0")])
                    P.op("dve", lambda e: e.tensor_tensor(out=t_obsq[:], in0=t_ob[:], in1=t_ob[:], op=ALU.mult),
                         reads=[buf("tc0")], writes=[buf("sq0")])
                u = 0 if typ == 1 else 1
                p0 = u * 64
                P.op("dve", lambda e: e.reciprocal(out=s_sb[0:64, :], in_=s_sb[0:64, :]),
                     reads=[buf("td")], writes=[buf("td")])
                if p0 == 0:
                    P.op("dve", lambda e: e.tensor_tensor(out=o_sb[0:64, :], in0=o_sb[0:64, :], in1=s_sb[0:64, :],
                                                          op=ALU.mult),
                         reads=[buf("tb1"), buf("td")], writes=[buf("tb1")])
                    src, srcB = o_sb, buf("tb1")
                else:
                    P.op("dve", lambda e: e.tensor_tensor(out=s_sb[64:128, :], in0=o_sb[0:64, :], in1=s_sb[0:64, :],
                                                          op=ALU.mult),
                         reads=[buf("tb1"), buf("td")], writes=[buf("td")])
                    src, srcB = s_sb, buf("td")
                P.op("dve", lambda e: e.scalar_tensor_tensor(
                    out=mixT[p0:p0 + 64, i, :], in0=src[p0:p0 + 64, :], scalar=0.5, in1=gate[p0:p0 + 64, i, :],
                    op0=ALU.mult, op1=ALU.mult),
                    reads=[srcB, buf("gate")], writes=[buf("mixT%d" % i)])

            def finalize_b2a(i, bm):
                P.op("pe", lambda e: e.matmul(bk(bm), lhsT=cmat[:, ONES, :], rhs=t_obsq[:], start=True, stop=True),
                     reads=[buf("sq0"), buf("cmat")], writes=[bank[bm]])

            def finalize_b2b(i, bm):
                rstd_from(bk(bm), t_b[0][:], 1.0 / 128.0, reads=[bank[bm]], writes=[buf("tb0")])

            def finalize_b2c(i, bm):
                obB = buf("tc0")
                rs = t_b[0]
                P.op("dve", lambda e: e.scalar_tensor_tensor(
                    out=t_ob[:], in0=t_ob[:], scalar=small[:, C_CSUB:C_CSUB + 1], in1=rs[:],
                    op0=ALU.mult, op1=ALU.mult),
                    reads=[obB, buf("tb0"), buf("small")], writes=[obB])
                P.op("dve", lambda e: e.scalar_tensor_tensor(
                    out=mixT[:, 4 + i, :], in0=t_ob[:], scalar=0.5, in1=gate[:, 4 + i, :],
                    op0=ALU.mult, op1=ALU.mult),
                    reads=[obB, buf("gate")], writes=[buf("mixT%d" % (4 + i))])

            nit = len(iters)
            emit_qk(0)
            if nit > 1:
                emit_qk(1)
            for n in range(nit):
                E_, EB_ = emit_exp(n)
                if n + 2 < nit:
                    emit_qk(n + 2)
                emit_pv(n, E_, EB_)
                for fn_ in deferred.pop(n, []):
                    fn_()
                pi, kb = iters[n]
                if kb == nkb - 1:
                    typ, i = pairs[pi]
                    finalize_pair(pi)
                    if typ == 2:
                        bm = a_bank(pi)
                        d0 = max(1, min(10, nkb - 3))
                        for dd, fn2 in ((d0, finalize_b2a), (d0 + 2, finalize_b2b), (d0 + 3, finalize_b2c)):
                            deferred.setdefault(min(n + dd, nit - 1), []).append(
                                lambda i=i, bm=bm, fn2=fn2: fn2(i, bm))
            for k_ in sorted(deferred):
                for fn_ in deferred[k_]:
                    fn_()

            if dbg == 3:
                return
            if blk + 1 < nblk:
                q_proj(blk + 1)
            c_order = [0, 4, 1, 5, 2, 6, 3, 7]
            xts = []
            for tt in range(4):
                xts.append(load_x(base + t0 + tt * 128))
            for ci, c in enumerate(c_order):
                for tt in range(4):
                    for half in range(2):
                        bb = tt * 2 + half
                        P.op("pe", lambda e, c=c, half=half, bb=bb, tt=tt, ci=ci: e.matmul(
                            bk(bb), lhsT=mixT[:, c, tt * 128:(tt + 1) * 128],
                            rhs=wout[:, c, half * 512:(half + 1) * 512], start=(ci == 0), stop=(ci == 7)),
                            reads=[buf("mixT%d" % c), buf("wout")], writes=[bank[bb]])
            def ep_add(tt):
                xb, xB = xts[tt]
                b0 = tt * 2
                hb, hB = hbfs[tt % 2], buf("hbf%d" % (tt % 2))
                cs, cr = 16 + tt, 20 + tt
                ssB, rsB = buf("ess%d" % tt), buf("erstd%d" % tt)
                P.op("dve", lambda e: e.tensor_tensor(out=xb[:], in0=ps[:, b0 * 512:(b0 + 2) * 512],
                                                      in1=xb[:], op=ALU.add),
                     reads=[bank[b0], bank[b0 + 1], xB], writes=[xB])
                P.op("act", lambda e: e.activation(out=hb[:], in_=xb[:], func=AF.Square,
                                                   accum_out=small[:, cs:cs + 1]),
                     reads=[xB], writes=[hB, ssB])
                rstd_from(small[:, cs:cs + 1], small[:, cr:cr + 1], 1.0 / D_MODEL, reads=[ssB], writes=[rsB])

            def ep_scale(tt):
                xb, xB = xts[tt]
                cr = 20 + tt
                row0 = base + t0 + tt * 128
                P.op("dve", lambda e: e.scalar_tensor_tensor(
                    out=xb[:], in0=xb[:], scalar=small[:, cr:cr + 1], in1=gfin[:],
                    op0=ALU.mult, op1=ALU.mult),
                    reads=[xB, buf("erstd%d" % tt), buf("gfin")], writes=[xB])
                store_ops.append(P.dma("pool", ys[row0:row0 + 128, :], xb[:], key="st_" + xB.name, reads=[xB]))

            ep_add(0)
            ep_add(1)
            ep_add(2)
            ep_scale(0)
            ep_add(3)
            ep_scale(1)
            ep_scale(2)
            ep_scale(3)

    store_ops = []
    base = 0
    for S in SEQS:
        do_sequence(base, S)
        base += S
    P.op("sp", lambda e: None, extra=store_ops)
    return nc, P


def _rope_tables():
    f = np.float32
    t = np.arange(SMAX)
    row = (t // GRID_W).astype(f)
    col = (t % GRID_W).astype(f)
    pos = t.astype(f)
    inv16 = (f(ROPE_THETA) ** (-(np.arange(0, 32, 2).astype(f)) / f(32))).astype(f)
    inv32 = (f(ROPE_THETA) ** (-(np.arange(0, 64, 2).astype(f)) / f(64))).astype(f)
    tabs = np.zeros((4, 128, SMAX), f)
    for p in range(128):
        d = p % 64
        i = (d % 32) % 16
        ps_ = row if d < 32 else col
        ang = (ps_ * inv16[i]).astype(f)
        tabs[0, p] = np.cos(ang)
        tabs[1, p] = np.sin(ang)
        i2 = d % 32
        ang2 = (pos * inv32[i2]).astype(f)
        tabs[2, p] = np.cos(ang2)
        tabs[3, p] = np.sin(ang2)
    return tabs


def _const_mats():
    f = np.float32
    cm = np.zeros((128, 5, 128), f)
    cm[:, 0, :] = np.eye(128, dtype=f)
    cm[:, 1, :] = 1.0
    for p in range(128):
        for m in range(128):
            if p // 64 == m // 64:
                cm[p, 2, m] = 1.0
    for m in range(128):
        d = m % 32
        if d < 16:
            cm[m + 16, 3, m] = -1.0
        else:
            cm[m - 16, 3, m] = 1.0
        d2 = m % 64
        if d2 < 32:
            cm[m + 32, 4, m] = -1.0
        else:
            cm[m - 32, 4, m] = 1.0
    return cm


def _pack_weights(w_in, w_out):
    QA0, KA0, VA0, GA0, QB0, KB0, VB0, GB0 = 0, 512, 640, 768, 1280, 1792, 2304, 2816
    cols = []
    cols.append(np.arange(KA0, KA0 + 128))
    for i in range(4):
        cols.append(np.arange(KB0 + i * 128, KB0 + (i + 1) * 128))
    cols.append(np.arange(VA0, VA0 + 128))
    for i in range(4):
        cols.append(np.arange(VB0 + i * 128, VB0 + (i + 1) * 128))
    for j in range(4):
        cols.append(np.concatenate([np.arange(QA0 + j * 64, QA0 + (j + 1) * 64),
                                    np.arange(QA0 + (j + 4) * 64, QA0 + (j + 5) * 64)]))
    for i in range(4):
        cols.append(np.arange(QB0 + i * 128, QB0 + (i + 1) * 128))
    for j in range(4):
        cols.append(np.concatenate([np.arange(GA0 + j * 64, GA0 + (j + 1) * 64),
                                    np.arange(GA0 + (j + 4) * 64, GA0 + (j + 5) * 64)]))
    for i in range(4):
        cols.append(np.arange(GB0 + i * 128, GB0 + (i + 1) * 128))
    assert len(cols) == N_CH
    w = w_in[0]
    wch = np.empty((N_CH, 128, 8, 128), np.float32)
    wr = w.reshape(8, 128, -1)
    for c, cl in enumerate(cols):
        wch[c] = wr[:, :, cl].transpose(1, 0, 2)
    rows = []
    for j in range(4):
        rows.append(np.concatenate([np.arange(j * 64, (j + 1) * 64), np.arange((j + 4) * 64, (j + 5) * 64)]))
    for hb in range(4):
        rows.append(np.arange(512 + hb * 128, 512 + (hb + 1) * 128))
    wo = w_out[0]
    wout = np.empty((128, 8, 1024), np.float32)
    for c, rw in enumerate(rows):
        wout[:, c, :] = wo[rw, :]
    return wch, wout


_CACHE = {}


def kernel(x_prompt, x_sample, g_norm, w_in, a_q_norm, a_k_norm, b_lambda_q1, b_lambda_k1,
           b_lambda_q2, b_lambda_k2, b_subln, w_out, g_final):
    f = np.float32
    x_prompt = np.asarray(x_prompt, f)
    x_sample = np.asarray(x_sample, f)
    wch, wout = _pack_weights(np.asarray(w_in, f), np.asarray(w_out, f))
    vecs = np.zeros((128, 16), f)
    vecs[:, 0:8] = np.asarray(g_norm, f)[0].reshape(8, 128).T
    vecs[:, 8] = np.tile(np.asarray(a_q_norm, f)[0], 2)
    vecs[:, 9] = np.tile(np.asarray(a_k_norm, f)[0], 2)
    vecs[:, 10] = np.asarray(b_subln, f)[0]
    lamv = np.stack([np.asarray(v, f)[0] for v in (b_lambda_q1, b_lambda_k1, b_lambda_q2, b_lambda_k2)], 0)
    lamv = np.ascontiguousarray(np.broadcast_to(lamv[None], (128, 4, 64)))
    gfin = np.ascontiguousarray(np.broadcast_to(np.asarray(g_final, f)[None, :], (128, 1024)))
    if "tabs" not in _CACHE:
        _CACHE["tabs"] = (_rope_tables(), _const_mats())
    rope, cmat = _CACHE["tabs"]

    in_maps = []
    for c in range(N_CORES):
        xs = np.concatenate([x_prompt[2 * c], x_prompt[2 * c + 1], x_sample[c]], axis=0)
        in_maps.append({"xs": np.ascontiguousarray(xs), "wch": wch, "wout": wout, "vecs": vecs,
                        "lamv": lamv, "gfin": gfin, "rope": rope, "cmat": cmat})
    nc, P = build_program()
    P.emit()
    P.close()
    res = run_bass_kernel_spmd(nc, in_maps, core_ids=list(range(N_CORES)))
    y_prompt = np.empty_like(x_prompt)
    y_sample = np.empty_like(x_sample)
    for c in range(N_CORES):
        ys = res.results[c]["ys"]
        y_prompt[2 * c] = ys[0:2048]
        y_prompt[2 * c + 1] = ys[2048:4096]
        y_sample[c] = ys[4096:8192]
    return (y_prompt, y_sample)
```
